# Optimizing a Trainium2 kernel written in Bass

```python
import jax
import jax.numpy as jnp
from jax import lax
import numpy as np


D_MODEL = 1024
BATCH = 8
SEQ = 4096
DEPTH = 2

NSA_HEADS = 8
NSA_GROUPS = 2
NSA_HEADS_PER_GROUP = NSA_HEADS // NSA_GROUPS
HEAD_DIM = 64
NSA_WIDTH = NSA_HEADS * HEAD_DIM
NSA_KV = NSA_GROUPS * HEAD_DIM
CMP_BLOCK = 32
CMP_STRIDE = 16
CMP_HIDDEN = 128
SLC_BLOCK = 64
SLC_TOPK = 16
WINDOW = 512
NSA_QBLOCK = 64
GLA_HEADS = 4
GLA_KEY_HEAD = 64
GLA_VALUE_HEAD = 128
GLA_KEY_DIM = GLA_HEADS * GLA_KEY_HEAD
GLA_VALUE_DIM = GLA_HEADS * GLA_VALUE_HEAD
GLA_GATE_RANK = 16
GLA_TAU = 16.0
GLA_CHUNK = 64
ROPE_THETA = 10000.0
NORM_EPS = 1e-6
N_BRANCHES = 2
IN_SPLITS = (NSA_WIDTH, NSA_KV, NSA_KV, NSA_KV, NSA_KV, NSA_KV, NSA_KV, 3 * NSA_HEADS, NSA_WIDTH,
             GLA_KEY_DIM, GLA_KEY_DIM, GLA_VALUE_DIM, GLA_GATE_RANK, GLA_VALUE_DIM, N_BRANCHES * D_MODEL)
D_IN = (2 * NSA_WIDTH + 6 * NSA_KV + 3 * NSA_HEADS + 2 * GLA_KEY_DIM + 2 * GLA_VALUE_DIM
        + GLA_GATE_RANK + N_BRANCHES * D_MODEL)

kernel_name = 'nsa_gla_gated_hybrid_trunk'


def _rms(x, g):
    xf = x.astype(jnp.float32)
    y = xf * lax.rsqrt(jnp.mean(xf * xf, axis=-1, keepdims=True) + NORM_EPS)
    return (y * g.astype(jnp.float32)).astype(x.dtype)


def _rope(x, cos, sin):
    x1, x2 = jnp.split(x.astype(jnp.float32), 2, axis=-1)
    c = cos[None, :, None, :]
    s = sin[None, :, None, :]
    return jnp.concatenate([x1 * c - x2 * s, x2 * c + x1 * s], axis=-1).astype(x.dtype)


def _masked_softmax(s, mask):
    s = jnp.where(mask, s.astype(jnp.float32), -jnp.inf)
    m = jnp.max(s, axis=-1, keepdims=True)
    m = jnp.where(jnp.isfinite(m), m, 0.0)
    e = jnp.where(mask, jnp.exp(s - m), 0.0)
    return e / jnp.maximum(jnp.sum(e, axis=-1, keepdims=True), 1e-30)


def _compress(kraw, pos_emb, w1, w2):
    S = kraw.shape[1]
    ncmp = (S - CMP_BLOCK) // CMP_STRIDE + 1
    idx = CMP_STRIDE * jnp.arange(ncmp)[:, None] + jnp.arange(CMP_BLOCK)[None, :]
    blocks = kraw[:, idx] + pos_emb[None, None, :, None, :]
    hid = jax.nn.silu(jnp.einsum('bjlgd,ldh->bjgh', blocks, w1))
    return jnp.einsum('bjgh,hd->bgjd', hid, w2)


def _nsa_attention(q_rope, q_nope, k_cmp, v_cmp, k_slc, v_slc, k_win, v_win, gate):
    B, S, G, HG, d = q_rope.shape
    scale = d ** -0.5
    ncmp = k_cmp.shape[2]
    nblk = S // SLC_BLOCK
    topk = min(SLC_TOPK, nblk)
    nq = S // NSA_QBLOCK
    cs = CMP_STRIDE * jnp.arange(ncmp)
    cmp_end = cs + CMP_BLOCK - 1
    bs = SLC_BLOCK * jnp.arange(nblk)
    overlap = ((cs[:, None] < bs[None, :] + SLC_BLOCK) & (cs[:, None] + CMP_BLOCK > bs[None, :])).astype(jnp.float32)
    ks_blk = k_slc.reshape(B, nblk, SLC_BLOCK, G, d).transpose(0, 3, 1, 2, 4)
    vs_blk = v_slc.reshape(B, nblk, SLC_BLOCK, G, d).transpose(0, 3, 1, 2, 4)
    kw_pad = jnp.pad(k_win, ((0, 0), (WINDOW, 0), (0, 0), (0, 0)))
    vw_pad = jnp.pad(v_win, ((0, 0), (WINDOW, 0), (0, 0), (0, 0)))
    b_ix = jnp.arange(B)[:, None, None, None]
    g_ix = jnp.arange(G)[None, :, None, None]
    blk_ids = jnp.arange(nblk)

    def query_block(i):
        start = i * NSA_QBLOCK
        t = start + jnp.arange(NSA_QBLOCK)
        qr = lax.dynamic_slice_in_dim(q_rope, start, NSA_QBLOCK, axis=1)
        qn = lax.dynamic_slice_in_dim(q_nope, start, NSA_QBLOCK, axis=1)
        gt = lax.dynamic_slice_in_dim(gate, start, NSA_QBLOCK, axis=1)
        s_c = jnp.einsum('bqghd,bgjd->bghqj', qn, k_cmp) * scale
        p_c = _masked_softmax(s_c, cmp_end[None, :] <= t[:, None])
        o_c = jnp.einsum('bghqj,bgjd->bqghd', p_c.astype(v_cmp.dtype), v_cmp)
        imp = jnp.einsum('bghqj,jn->bgqn', p_c, overlap)
        cur = t // SLC_BLOCK
        forced = ((blk_ids[None, :] == 0) | (blk_ids[None, :] == cur[:, None])
                  | (blk_ids[None, :] == cur[:, None] - 1))
        valid = bs[None, :] <= t[:, None]
        imp = jnp.where(forced, jnp.inf, jnp.where(valid, imp, -jnp.inf))
        _, sel = lax.top_k(imp, topk)
        k_sel = ks_blk[b_ix, g_ix, sel]
        v_sel = vs_blk[b_ix, g_ix, sel]
        kpos = sel[..., None] * SLC_BLOCK + jnp.arange(SLC_BLOCK)
        m_s = (kpos <= t[None, None, :, None, None]).reshape(B, G, 1, NSA_QBLOCK, topk * SLC_BLOCK)
        s_s = jnp.einsum('bqghd,bgqkld->bghqkl', qr, k_sel).reshape(B, G, HG, NSA_QBLOCK, topk * SLC_BLOCK) * scale
        p_s = _masked_softmax(s_s, m_s).reshape(B, G, HG, NSA_QBLOCK, topk, SLC_BLOCK)
        o_s = jnp.einsum('bghqkl,bgqkld->bqghd', p_s.astype(v_sel.dtype), v_sel)
        kwin = lax.dynamic_slice_in_dim(kw_pad, start, WINDOW + NSA_QBLOCK, axis=1)
        vwin = lax.dynamic_slice_in_dim(vw_pad, start, WINDOW + NSA_QBLOCK, axis=1)
        kp = start - WINDOW + jnp.arange(WINDOW + NSA_QBLOCK)
        m_w = (kp[None, :] >= 0) & (kp[None, :] <= t[:, None]) & (kp[None, :] > t[:, None] - WINDOW)
        s_w = jnp.einsum('bqghd,bkgd->bghqk', qr, kwin) * scale
        p_w = _masked_softmax(s_w, m_w)
        o_w = jnp.einsum('bghqk,bkgd->bqghd', p_w.astype(vwin.dtype), vwin)
        return gt[..., 0:1] * o_c + gt[..., 1:2] * o_s + gt[..., 2:3] * o_w

    out = lax.map(query_block, jnp.arange(nq))
    return out.transpose(1, 0, 2, 3, 4, 5).reshape(B, S, G * HG * d)


def _gla(q, k, v, log_a):
    B, S, H, dk = q.shape
    dv = v.shape[-1]
    C = GLA_CHUNK
    nc = S // C
    f = lambda z: z.astype(jnp.float32).reshape(B, nc, C, H, -1).transpose(0, 3, 1, 2, 4)
    q = f(q) * dk ** -0.5
    k = f(k)
    v = f(v)
    b = jnp.cumsum(f(log_a), axis=3)
    b_last = b[:, :, :, -1:, :]
    qg = q * jnp.exp(b)
    kg = k * jnp.exp(-b)
    causal = jnp.tril(jnp.ones((C, C), dtype=bool))
    att = jnp.where(causal, jnp.einsum('bhnid,bhnjd->bhnij', qg, kg), 0.0)
    o_intra = jnp.einsum('bhnij,bhnjv->bhniv', att, v)
    upd = jnp.einsum('bhncd,bhncv->bhndv', k * jnp.exp(b_last - b), v)
    decay = jnp.exp(b_last[:, :, :, 0, :])

    def step(state, inp):
        dec, u = inp
        return dec[..., None] * state + u, state

    _, s_prev = lax.scan(step, jnp.zeros((B, H, dk, dv), jnp.float32),
                         (decay.transpose(2, 0, 1, 3), upd.transpose(2, 0, 1, 3, 4)))
    s_prev = s_prev.transpose(1, 2, 0, 3, 4)
    o = o_intra + jnp.einsum('bhncd,bhndv->bhncv', qg, s_prev)
    return o.transpose(0, 2, 3, 1, 4).reshape(B, S, H, dv)


def _layer(x, cos, sin, g_pre, w_in, cmp_pos_k, cmp_w1_k, cmp_w2_k, cmp_pos_v, cmp_w1_v, cmp_w2_v,
           gla_w_a, gla_b_a, gla_g_norm, w_up_nsa, w_up_gla, w_out, g_post):
    B, S, _ = x.shape
    G, HG, d = NSA_GROUPS, NSA_HEADS_PER_GROUP, HEAD_DIM
    h = _rms(x, g_pre)
    proj = h @ w_in
    offs = np.cumsum(IN_SPLITS)[:-1].tolist()
    (p_q, p_kc, p_vc, p_ks, p_vs, p_kw, p_vw, p_ng, p_nz,
     p_gq, p_gk, p_gv, p_ga, p_gr, p_mg) = jnp.split(proj, offs, axis=-1)
    q = p_q.reshape(B, S, NSA_HEADS, d)
    q_rope = _rope(q, cos, sin).reshape(B, S, G, HG, d)
    q_nope = q.reshape(B, S, G, HG, d)
    k_cmp = _compress(p_kc.reshape(B, S, G, d), cmp_pos_k, cmp_w1_k, cmp_w2_k)
    v_cmp = _compress(p_vc.reshape(B, S, G, d), cmp_pos_v, cmp_w1_v, cmp_w2_v)
    k_slc = _rope(p_ks.reshape(B, S, G, d), cos, sin)
    v_slc = p_vs.reshape(B, S, G, d)
    k_win = _rope(p_kw.reshape(B, S, G, d), cos, sin)
    v_win = p_vw.reshape(B, S, G, d)
    nsa_gate = jax.nn.sigmoid(p_ng).reshape(B, S, G, HG, 3)
    o_nsa = _nsa_attention(q_rope, q_nope, k_cmp, v_cmp, k_slc, v_slc, k_win, v_win, nsa_gate)
    o_nsa = o_nsa * jax.nn.silu(p_nz)
    gq = p_gq.reshape(B, S, GLA_HEADS, GLA_KEY_HEAD)
    gk = p_gk.reshape(B, S, GLA_HEADS, GLA_KEY_HEAD)
    gv = p_gv.reshape(B, S, GLA_HEADS, GLA_VALUE_HEAD)
    a_logit = (p_ga @ gla_w_a + gla_b_a).astype(jnp.float32)
    log_a = (jax.nn.log_sigmoid(a_logit) / GLA_TAU).reshape(B, S, GLA_HEADS, GLA_KEY_HEAD)
    o_gla = _rms(_gla(gq, gk, gv, log_a), gla_g_norm).astype(x.dtype)
    o_gla = (o_gla * jax.nn.silu(p_gr.reshape(B, S, GLA_HEADS, GLA_VALUE_HEAD))).reshape(B, S, GLA_VALUE_DIM)
    m_nsa, m_gla = jnp.split(jax.nn.sigmoid(p_mg), N_BRANCHES, axis=-1)
    y = m_nsa * (o_nsa @ w_up_nsa) + m_gla * (o_gla @ w_up_gla)
    out = y @ w_out
    return x + _rms(out, g_post)


def setup_inputs(seed: int = 0) -> dict:
    key = jax.random.key(seed)
    ks = jax.random.split(key, 16)
    L, D = DEPTH, D_MODEL
    nrm = lambda k, shape, fan_in: jax.random.normal(k, shape, jnp.float32) * fan_in ** -0.5
    small = lambda k, shape, s: s * jax.random.normal(k, shape, jnp.float32)
    return {
        'x': jax.random.normal(ks[0], (BATCH, SEQ, D), jnp.float32),
        'g_pre': 1.0 + small(ks[1], (L, D), 0.01),
        'w_in': nrm(ks[2], (L, D, D_IN), D),
        'cmp_pos_k': small(ks[3], (L, CMP_BLOCK, HEAD_DIM), 0.02),
        'cmp_w1_k': nrm(ks[4], (L, CMP_BLOCK, HEAD_DIM, CMP_HIDDEN), CMP_BLOCK * HEAD_DIM),
        'cmp_w2_k': nrm(ks[5], (L, CMP_HIDDEN, HEAD_DIM), CMP_HIDDEN),
        'cmp_pos_v': small(ks[6], (L, CMP_BLOCK, HEAD_DIM), 0.02),
        'cmp_w1_v': nrm(ks[7], (L, CMP_BLOCK, HEAD_DIM, CMP_HIDDEN), CMP_BLOCK * HEAD_DIM),
        'cmp_w2_v': nrm(ks[8], (L, CMP_HIDDEN, HEAD_DIM), CMP_HIDDEN),
        'gla_w_a': nrm(ks[9], (L, GLA_GATE_RANK, GLA_KEY_DIM), GLA_GATE_RANK),
        'gla_b_a': small(ks[10], (L, GLA_KEY_DIM), 0.1),
        'gla_g_norm': 1.0 + small(ks[11], (L, GLA_VALUE_HEAD), 0.01),
        'w_up_nsa': nrm(ks[12], (L, NSA_WIDTH, D), NSA_WIDTH),
        'w_up_gla': nrm(ks[13], (L, GLA_VALUE_DIM, D), GLA_VALUE_DIM),
        'w_out': nrm(ks[14], (L, D, D), D),
        'g_post': 1.0 + small(ks[15], (L, D), 0.01),
    }


def reference(x, g_pre, w_in, cmp_pos_k, cmp_w1_k, cmp_w2_k, cmp_pos_v, cmp_w1_v, cmp_w2_v,
              gla_w_a, gla_b_a, gla_g_norm, w_up_nsa, w_up_gla, w_out, g_post):
    S = x.shape[1]
    pos = jnp.arange(S, dtype=jnp.float32)
    inv_freq = ROPE_THETA ** (-jnp.arange(0, HEAD_DIM, 2, dtype=jnp.float32) / HEAD_DIM)
    ang = pos[:, None] * inv_freq[None, :]
    cos, sin = jnp.cos(ang), jnp.sin(ang)
    for l in range(DEPTH):
        x = _layer(x, cos, sin, g_pre[l], w_in[l], cmp_pos_k[l], cmp_w1_k[l], cmp_w2_k[l],
                   cmp_pos_v[l], cmp_w1_v[l], cmp_w2_v[l], gla_w_a[l], gla_b_a[l], gla_g_norm[l],
                   w_up_nsa[l], w_up_gla[l], w_out[l], g_post[l])
    return x
```

```python
import numpy as np
from contextlib import ExitStack
import concourse.bass as bass
import concourse.mybir as mybir
from concourse.bass_utils import run_bass_kernel_spmd

F32 = mybir.dt.float32
BF16 = mybir.dt.bfloat16
AF = mybir.ActivationFunctionType
ALU = mybir.AluOpType
AX = mybir.AxisListType

S = 4096
D = 1024
NT = S // 128
NQ = S // 512
DIN = 5416
DEPTH = 2
BIG = 30000.0
EPS = 1e-6
C_Q, C_KC, C_VC, C_KS, C_VS, C_KW, C_VW, C_NG, C_NZ, C_GQ, C_GK, C_GV, C_GA, C_GR, C_MG = (
    0, 512, 640, 768, 896, 1024, 1152, 1280, 1304, 1816, 2072, 2328, 2840, 2856, 3368)


class Op:
    __slots__ = ("eng", "fn", "deps", "signal", "sigval", "dma", "dmaval", "idx", "bar")

    def __init__(self, eng, fn):
        self.eng = eng; self.fn = fn; self.deps = []; self.signal = False; self.sigval = None
        self.dma = None; self.dmaval = None; self.bar = None


class Sched:
    ENGS = ("pe", "act", "dve", "pool", "sp")

    def __init__(self, nc, es):
        self.nc = nc
        self.esem = {e: es.enter_context(nc.semaphore("sem_" + e)) for e in ("pe", "act", "dve", "pool")}
        self.ecount = {e: 0 for e in self.esem}
        self.dsem_free = [es.enter_context(nc.semaphore("dsem%d" % i)) for i in range(60)]
        self.dsem_free_sw = [es.enter_context(nc.semaphore("qsem%d" % i)) for i in range(24)]
        self.dsem_cnt = {id(s): 0 for s in self.dsem_free + self.dsem_free_sw}
        self.dkey = {}
        self.waited = {e: {} for e in self.ENGS}
        self.reset_phase()

    def is_excl(self, k):
        if isinstance(k, tuple): k = k[0]
        return isinstance(k, str) and k.startswith("ps:")

    def reset_phase(self):
        self.ops = {e: [] for e in self.ENGS}
        self.last_w = {}
        self.readers = {}
        for k, s in self.dkey.items():
            (self.dsem_free_sw if k[1] else self.dsem_free).append(s)
        self.dkey = {}

    def add(self, eng, meth, r=(), w=(), dma=None, **kw):
        op = Op(eng, (meth, kw))
        raw = set(); other = set()
        for k in r:
            x = self.last_w.get(k)
            if x is not None: raw.add(x)
            if self.is_excl(k):
                for y in self.readers.get(k, ()):
                    if y.eng != eng: raw.add(y)
        for k in w:
            x = self.last_w.get(k)
            if x is not None: raw.add(x)
            for y in self.readers.get(k, ()): other.add(y)
        for k in w:
            self.last_w[k] = op; self.readers[k] = []
        for k in r:
            self.readers.setdefault(k, []).append(op)
        deps = []
        for d in raw | other:
            if d is op: continue
            if d.dma is None and d.eng == eng:
                if eng == "pe": continue
                if d not in raw: continue
            deps.append(d)
            if d.dma is None: d.signal = True
        op.deps = deps
        if dma is not None:
            dma = (dma, eng == "pool")
            s = self.dkey.get(dma)
            if s is None:
                s = (self.dsem_free_sw if dma[1] else self.dsem_free).pop(); self.dkey[dma] = s
            self.dsem_cnt[id(s)] += 16
            op.dma = s; op.dmaval = self.dsem_cnt[id(s)]
        self.ops[eng].append(op)
        return op

    def emit(self, name):
        nc = self.nc
        lasts = {}
        for e in self.esem:
            real = [o for o in self.ops[e] if o.dma is None]
            if real:
                real[-1].signal = True
                lasts[e] = real[-1]
        for e in self.esem:
            for o in self.ops[e]:
                if o.dma is None and o.signal:
                    self.ecount[e] += 1; o.sigval = self.ecount[e]
        dma_totals = [(s, self.dsem_cnt[id(s)]) for s in self.dkey.values()]
        bar = [(self.esem[e], lasts[e].sigval) for e in lasts] + dma_totals

        def run(ename, eng):
            wd = self.waited[ename]

            def wait(sem, val):
                if wd.get(id(sem), 0) >= val: return
                wd[id(sem)] = val
                eng.wait_ge(sem, val)
            for o in self.ops[ename]:
                for d in o.deps:
                    if d.dma is not None: wait(d.dma, d.dmaval)
                    else: wait(self.esem[d.eng], d.sigval)
                ins = getattr(eng, o.fn[0])(**o.fn[1])
                if o.dma is not None: ins.then_inc(o.dma, 16)
                elif o.signal: ins.then_inc(self.esem[ename], 1)
            for sem, val in bar:
                wait(sem, val)
        with nc.Block() as block:
            @block.tensor
            def _(e): run("pe", e)

            @block.scalar
            def _(e): run("act", e)

            @block.vector
            def _(e): run("dve", e)

            @block.gpsimd
            def _(e): run("pool", e)

            @block.sync
            def _(e): run("sp", e)
        self.reset_phase()


class Ring:
    def __init__(self, tiles, name):
        self.tiles = tiles; self.i = 0; self.name = name

    def next(self):
        t = self.tiles[self.i % len(self.tiles)]; k = (self.name, self.i % len(self.tiles)); self.i += 1
        return t, k


class Ctx:
    def __init__(self, nc, es):
        self.nc = nc; self.S = Sched(nc, es); self.uid = 0; self.pes = None

    def sb(self, shape, dt, name=None):
        self.uid += 1
        return self.pes.enter_context(self.nc.sbuf_tensor("%s_%d" % (name or "t", self.uid), list(shape), dt))

    def lsb(self, shape, dt, name=None):
        self.uid += 1
        return self.les.enter_context(self.nc.sbuf_tensor("%s_%d" % (name or "t", self.uid), list(shape), dt))

    def asb(self, shape, dt, name=None):
        self.uid += 1
        return self.aes.enter_context(self.nc.sbuf_tensor("%s_%d" % (name or "t", self.uid), list(shape), dt))

    def ps(self, shape, dt, name=None):
        self.uid += 1
        return self.pes.enter_context(self.nc.psum_tensor("%s_%d" % (name or "p", self.uid), list(shape), dt))

    def ring(self, n, shape, dt, name):
        return Ring([self.sb(shape, dt, name) for _ in range(n)], name + str(self.uid))

    def psring(self, n, shape, dt, name):
        return Ring([self.ps(shape, dt, name) for _ in range(n)], "ps:" + name + str(self.uid))


def phase_proj(cx, l, xsrc, W, C, SC):
    A = cx.S.add
    nc = cx.nc
    ident = cx.sb([128, 128], BF16, "ident")
    A("sp", "dma_start", out=ident[:, :], in_=C["ident"][:, :], w=["ident"], dma="ident")
    eps_t = cx.sb([128, 1], F32, "eps")
    A("pool", "memset", ap=eps_t[:, :], constant=EPS, w=["eps"])
    gbc = cx.sb([128, D], F32, "gbc")
    A("sp", "dma_start", out=gbc[:, :], in_=W["g_pre"][l:l + 1, :].partition_broadcast(128)[:, 0, :], w=["gbc"], dma="gbc")
    hT = cx.sb([128, 8, S], BF16, "hT")
    cosT = cx.sb([128, S], F32, "cosT")
    sinS = cx.sb([128, S], F32, "sinS")
    xr = cx.ring(3, [128, D], F32, "x")
    hr = cx.ring(2, [128, D], BF16, "h")
    sq = cx.sb([128, D], BF16, "sq")
    st = cx.ring(4, [128, 4], F32, "st")
    pbf = cx.psring(2, [128, 1024], BF16, "pbf")
    P1 = [dict() for _ in range(NT)]

    def p1_s0(t):
        d = P1[t]
        d["x"] = xr.next(); d["s"] = st.next()
        xt, xk = d["x"]; s, sk = d["s"]
        A("sp", "dma_start", out=xt[:, :], in_=xsrc[t * 128:(t + 1) * 128, :], w=[xk], dma=xk)
        A("act", "activation", out=sq[:, :], in_=xt[:, :], func=AF.Square, accum_out=s[:, 0:1],
          r=[xk], w=["sq", (sk, 0)])
        A("act", "activation", out=s[:, 1:2], in_=s[:, 0:1], func=AF.Ln, scale=1.0 / D, bias=eps_t[:, 0:1],
          r=[(sk, 0), "eps"], w=[(sk, 1)])
        A("act", "activation", out=s[:, 2:3], in_=s[:, 1:2], func=AF.Exp, scale=-0.5,
          r=[(sk, 1)], w=[(sk, 2)])

    def p1_s1(t):
        d = P1[t]
        xt, xk = d["x"]; s, sk = d["s"]
        d["h"] = hr.next(); ht, hk = d["h"]
        A("dve", "scalar_tensor_tensor", out=ht[:, :], in0=xt[:, :], scalar=s[:, 2:3], in1=gbc[:, :], op0=ALU.mult, op1=ALU.mult,
          r=[xk, (sk, 2), "gbc"], w=[hk])

    def p1_s2(t):
        d = P1[t]
        ht, hk = d["h"]
        d["p"] = pbf.next(); pt, pk = d["p"]
        for kc in range(8):
            A("pe", "transpose", out=pt[:, kc * 128:(kc + 1) * 128], in_=ht[:, kc * 128:(kc + 1) * 128], identity=ident[:, :],
              r=[hk, "ident"], w=[pk])

    def p1_s3(t):
        pt, pk = P1[t]["p"]
        A("act", "activation", out=hT[:, :, t * 128:(t + 1) * 128], in_=pt[:, :].rearrange("p (k c) -> p k c", k=8), func=AF.Copy,
          r=[pk], w=[("hT", t)])
        P1[t].clear()

    def emit_p1():
        stages = (p1_s0, p1_s1, p1_s2, p1_s3)
        for step in range(NT + len(stages) - 1):
            for si, fn in enumerate(stages):
                t = step - si
                if 0 <= t < NT: fn(t)
            if step == 1:
                A("sp", "dma_start", out=cosT[:, :], in_=C["cosT"][:, :], w=["cosT"], dma="cosT")
                A("sp", "dma_start", out=sinS[:, :], in_=C["sinS"][:, :], w=["sinS"], dma="sinS")
            if step == 4:
                G[1][0]()
    stg = cx.ring(2, [128, 8, 528], F32, "stg")
    wbf = cx.ring(4, [128, 8, 536], BF16, "wbf")
    ps = cx.psring(6, [128, 512], F32, "pj")
    ost = cx.ring(6, [128, 512], BF16, "ost")
    tmp = cx.ring(4, [128, 512], F32, "ropet")
    gst = cx.ring(2, [128, 24], F32, "gst")
    vst = cx.ring(2, [128, 260], BF16, "vst")
    for _ in range(2):
        vt_, vk_ = vst.next()
        A("pool", "memset", ap=vt_[:, :], constant=1.0, w=[vk_])
    w_in = W["w_in"]
    hT_all = [("hT", t) for t in range(NT)]

    def load_stage(c0, n):
        sg, sgk = stg.next()
        A("sp", "dma_start", out=sg[:, :, 0:n], in_=w_in[l, :, c0:c0 + n].rearrange("(k p) c -> p k c", p=128),
          w=[sgk], dma=sgk)
        return sg, sgk

    def cast(wt, wk, d0, sg, sgk, s0, n, first):
        A("pool", "tensor_copy", out=wt[:, :, d0:d0 + n], in_=sg[:, :, s0:s0 + n], r=[sgk], w=[wk])

    def cast_perm(wt, wk, d0, sg, sgk, s0, nh):
        for half in range(2):
            A("pool", "tensor_copy",
                out=wt[:, :, d0:d0 + nh * 64].rearrange("p k (h two c) -> p k h two c", two=2, c=32)[:, :, :, half, :],
                in_=sg[:, :, s0:s0 + nh * 64].rearrange("p k (h two c) -> p k h two c", two=2, c=32)[:, :, :, 1 - half, :],
              r=[sgk], w=[wk])

    def fm_mm(wt, wk, c0, M, j, pt, pk):
        for kc in range(8):
            A("pe", "matmul", out=pt[0:M, :], lhsT=wt[:, kc, c0:c0 + M], rhs=hT[:, kc, j * 512:(j + 1) * 512],
                                             start=(kc == 0), stop=(kc == 7),
              r=[wk] + hT_all[4 * j:4 * j + 4], w=[pk])

    def tm_mm(wt, wk, c0, n, t, pt, pk):
        for kc in range(8):
            A("pe", "matmul", out=pt[:, 0:n], lhsT=hT[:, kc, t * 128:(t + 1) * 128], rhs=wt[:, kc, c0:c0 + n],
                                             start=(kc == 0), stop=(kc == 7),
              r=[wk, hT_all[t]], w=[pk])

    def store(dst_ap, o, ok, M, n, eng="act"):
        A(eng, "dma_start", out=dst_ap, in_=o[0:M, 0:n], r=[ok], w=[], dma=ok)

    def fm_copy(wt, wk, c0, M, dst, scale=None, func=None):
        for j in range(NQ):
            pt, pk = ps.next(); o, ok = ost.next()
            fm_mm(wt, wk, c0, M, j, pt, pk)
            if False:
                pass
            else:
                A("act", "activation", out=o[0:M, :], in_=pt[0:M, :], func=func or AF.Copy,
                                                          scale=1.0 if scale is None else scale, r=[pk], w=[ok])
            store(dst[:, j * 512:(j + 1) * 512], o, ok, M, 512)

    def fm_rope(wt, wk, c0, cp0, dst_r, dst_n, js_=None):
        for j in (range(NQ) if js_ is None else js_):
            p1, k1 = ps.next(); p2, k2 = ps.next()
            fm_mm(wt[0], wk[0], c0, 128, j, p1, k1)
            fm_mm(wt[1], wk[1], cp0, 128, j, p2, k2)
            js = slice(j * 512, (j + 1) * 512)
            if dst_n is not None:
                o, ok = ost.next()
                A("act", "activation", out=o[:, :], in_=p1[:, :], func=AF.Copy, r=[k1], w=[ok])
                store(dst_n[:, js], o, ok, 128, 512, "act")
            t1, tk1 = tmp.next(); t2, tk2 = tmp.next(); o, ok = ost.next()
            A("dve", "tensor_tensor", out=t1[:, :], in0=p1[:, :], in1=cosT[:, js], op=ALU.mult,
              r=[k1, "cosT"], w=[tk1])
            A("dve", "tensor_tensor", out=t2[:, :], in0=p2[:, :], in1=sinS[:, js], op=ALU.mult,
              r=[k2, "sinS"], w=[tk2])
            A("pool", "tensor_tensor", out=o[:, :], in0=t1[:, :], in1=t2[:, :], op=ALU.add,
              r=[tk1, tk2], w=[ok])
            store(dst_r[:, js], o, ok, 128, 512, "pool")

    G = []
    st_ = {}

    def g1_prep():
        sg, sgk = load_stage(C_Q, 512)
        wq, wqk = wbf.next(); wp, wpk = wbf.next()
        cast(wq, wqk, 0, sg, sgk, 0, 512, True)
        cast_perm(wp, wpk, 0, sg, sgk, 0, 8)
        st_["g1"] = (wq, wqk, wp, wpk)

    def g1_comp_j(j):
        wq, wqk, wp, wpk = st_["g1"]
        for hp2 in range(4):
            fm_rope((wq, wp), (wqk, wpk), hp2 * 128, hp2 * 128, SC["qrT"][2 * hp2:2 * hp2 + 2].rearrange("h d s -> (h d) s"),
                    SC["qnT"][2 * hp2:2 * hp2 + 2].rearrange("h d s -> (h d) s"), js_=[j])
    G.append((g1_prep, lambda: [g1_comp_j(j) for j in range(NQ)]))

    def g2_prep():
        sga, sgak = load_stage(C_KC, 512)
        sgb, sgbk = load_stage(C_KW, 280)
        w1, w1k = wbf.next()
        cast(w1, w1k, 0, sga, sgak, 0, 384, True)
        cast_perm(w1, w1k, 384, sga, sgak, 256, 2)
        w2, w2k = wbf.next()
        cast(w2, w2k, 0, sgb, sgbk, 0, 128, True)
        cast_perm(w2, w2k, 128, sgb, sgbk, 0, 2)
        cast(w2, w2k, 256, sga, sgak, 384, 128, False)
        cast(w2, w2k, 384, sgb, sgbk, 128, 152, False)
        st_["g2"] = (w1, w1k, w2, w2k)

    def g2_comp():
        w1, w1k, w2, w2k = st_["g2"]
        fm_copy(w1, w1k, 0, 128, SC["kcT"].rearrange("g d s -> (g d) s"))
        fm_copy(w1, w1k, 128, 128, SC["vcT"].rearrange("g d s -> (g d) s"))
        fm_rope((w1, w1), (w1k, w1k), 256, 384, SC["ksT"].rearrange("g d s -> (g d) s"), None)
        fm_rope((w2, w2), (w2k, w2k), 0, 128, SC["kwT"].rearrange("g d s -> (g d) s"), None)
        for t in range(NT):
            pt, pk = ps.next(); gt, gk = gst.next()
            tm_mm(w2, w2k, 256, 280, t, pt, pk)
            vt, vk = vst.next()
            A("act", "activation", out=vt[:, :].rearrange("p (f c) -> p f c", c=65)[:, :, 0:64],
              in_=pt[:, 0:256].rearrange("p (f c) -> p f c", c=64), func=AF.Copy, r=[pk], w=[vk])
            A("act", "activation", out=gt[:, :], in_=pt[:, 256:280], func=AF.Sigmoid, r=[pk], w=[gk])
            A("act", "dma_start", out=SC["vsw"][t * 128:(t + 1) * 128, :], in_=vt[:, :], r=[vk], dma=vk)
            A("act", "dma_start", out=SC["gate"][t * 128:(t + 1) * 128, :], in_=gt[:, :], r=[gk], dma=gk)
    G.append((g2_prep, g2_comp))

    def mk_tm(c0, n, off, name, func):
        key = "tm" + name

        def prep():
            sg, sgk = load_stage(c0, n)
            wt, wk = wbf.next()
            cast(wt, wk, 0, sg, sgk, off, 512, True)
            st_[key] = (wt, wk, sg, sgk)

        def comp():
            wt, wk, sg, sgk = st_[key]
            if name == "grs":
                fm_copy(wt, wk, 512, 16, SC["gaT"])
            for t in range(NT):
                pt, pk = ps.next(); o, ok = ost.next()
                tm_mm(wt, wk, 0, 512, t, pt, pk)
                A("act", "activation", out=o[:, :], in_=pt[:, :], func=(func or AF.Copy), r=[pk], w=[ok])
                store(SC[name][t * 128:(t + 1) * 128, :], o, ok, 128, 512)
        if name == "grs":
            def prep2():
                prep()
                wt, wk, sg, sgk = st_[key]
                cast(wt, wk, 512, sg, sgk, 0, 16, True)
            return prep2, comp
        return prep, comp
    for args in ((C_NZ, 512, 0, "nzs", AF.Silu), (C_GV, 512, 0, "gv", None), (C_GA, 528, 16, "grs", AF.Silu)):
        G.append(mk_tm(*args))

    def g4_prep():
        sg, sgk = load_stage(C_GQ, 512)
        wt, wk = wbf.next()
        cast(wt, wk, 0, sg, sgk, 0, 512, True)
        st_["g4"] = (wt, wk)

    def g4_comp():
        wt, wk = st_["g4"]
        for b in range(2):
            fm_copy(wt, wk, b * 128, 128, SC["gqT"][b * 128:(b + 1) * 128, :], scale=0.125)
            fm_copy(wt, wk, 256 + b * 128, 128, SC["gkT"][b * 128:(b + 1) * 128, :])
    G.append((g4_prep, g4_comp))

    def mk_mg(q4):
        def prep():
            sg, sgk = load_stage(C_MG + q4 * 512, 512)
            wt, wk = wbf.next()
            cast(wt, wk, 0, sg, sgk, 0, 512, True)
            st_["mg%d" % q4] = (wt, wk)

        def comp():
            wt, wk = st_["mg%d" % q4]
            for b in range(4):
                r0 = q4 * 512 + b * 128
                fm_copy(wt, wk, b * 128, 128, SC["mgT"][r0:r0 + 128, :], func=AF.Sigmoid)
        return prep, comp
    for q4 in range(4):
        G.append(mk_mg(q4))

    G[0][0]()
    emit_p1()
    for n, (prep, comp) in enumerate(G):
        if n >= 1 and n + 1 < len(G): G[n + 1][0]()
        comp()


def phase_compress(cx, l, W, C, SC):
    A = cx.S.add
    ps = cx.psring(4, [128, 512], F32, "cps")
    specs = (("kcT", "cmp_pos_k", "cmp_w1_k", "cmp_w2_k"), ("vcT", "cmp_pos_v", "cmp_w1_v", "cmp_w2_v"))
    T = []
    for kv, (src, posn, w1n, w2n) in enumerate(specs):
        d = {}
        w1f = cx.sb([64, 32, 128], F32, "w1f"); d["w1b"] = cx.sb([64, 32, 128], BF16, "w1b")
        pf = cx.sb([64, 32], F32, "pf"); d["pb"] = cx.sb([64, 32], BF16, "pb")
        w2f = cx.sb([128, 64], F32, "w2f"); d["w2b"] = cx.sb([128, 64], BF16, "w2b")
        d["cst"] = cx.sb([128, 1], F32, "cst")
        k = lambda s, kv=kv: "%s%d" % (s, kv)
        d["k"] = k
        A("sp", "dma_start", out=w1f[:, :, :], in_=W[w1n][l], w=[k("w1f")], dma=k("w1f"))
        A("sp", "dma_start", out=pf[:, :], in_=W[posn][l], w=[k("pf")], dma=k("pf"))
        A("sp", "dma_start", out=w2f[:, :], in_=W[w2n][l], w=[k("w2f")], dma=k("w2f"))
        d["kt"] = []
        for g in range(2):
            kt = cx.sb([64, S], BF16, "kt"); kk = "kt%d%d" % (kv, g)
            A("sp", "dma_start", out=kt[:, :], in_=SC[src][g], w=[kk], dma=kk)
            d["kt"].append((kt, kk))
        A("pool", "tensor_copy", out=d["w1b"][:, :, :], in_=w1f[:, :, :], r=[k("w1f")], w=[k("w1b")])
        A("pool", "tensor_copy", out=d["pb"][:, :], in_=pf[:, :], r=[k("pf")], w=[k("pb")])
        A("pool", "tensor_copy", out=d["w2b"][:, :], in_=w2f[:, :], r=[k("w2f")], w=[k("w2b")])
        T.append(d)
    for kv, d in enumerate(T):
        k = d["k"]; w1b = d["w1b"]; pb = d["pb"]; w2b = d["w2b"]; cst = d["cst"]
        pt, pk = ps.next()
        for li in range(32):
            A("pe", "matmul", out=pt[:, 0:1], lhsT=w1b[:, li, :], rhs=pb[:, li:li + 1], start=(li == 0), stop=(li == 31),
              r=[k("w1b"), k("pb")], w=[pk])
        A("dve", "tensor_copy", out=cst[:, :], in_=pt[:, 0:1], r=[pk], w=[k("cst")])
        for g in range(2):
            kt, kk = d["kt"][g]
            pt, pk = ps.next()
            for li in range(32):
                A("pe", "matmul", out=pt[:, 0:255], lhsT=w1b[:, li, :], rhs=kt[:, li:li + 16 * 254 + 1:16],
                  start=(li == 0), stop=(li == 31), r=[k("w1b"), kk], w=[pk])
            hb = cx.sb([128, 256], BF16, "hb"); hk = "hb%d%d" % (kv, g)
            A("act", "activation", out=hb[:, 0:255], in_=pt[:, 0:255], func=AF.Silu, bias=cst[:, 0:1], r=[pk, k("cst")], w=[hk])
            if kv == 0:
                p2, p2k = ps.next()
                A("pe", "matmul", out=p2[0:64, 0:255], lhsT=w2b[:, :], rhs=hb[:, 0:255], start=True, stop=True, r=[k("w2b"), hk], w=[p2k])
                o = cx.sb([64, 256], BF16, "kco"); ok = "kco%d" % g
                A("act", "activation", out=o[:, 0:255], in_=p2[0:64, 0:255], func=AF.Copy, r=[p2k], w=[ok])
                A("act", "dma_start", out=SC["kcmpT"][g][:, 0:255], in_=o[:, 0:255], r=[ok], dma=ok)
            else:
                for jt, nj in ((0, 128), (1, 127)):
                    p2, p2k = ps.next()
                    A("pe", "matmul", out=p2[0:nj, 0:64], lhsT=hb[:, jt * 128:jt * 128 + nj], rhs=w2b[:, :], start=True, stop=True,
                      r=[k("w2b"), hk], w=[p2k])
                    o = cx.sb([128, 64], BF16, "vco"); ok = "vco%d%d" % (g, jt)
                    A("act", "activation", out=o[0:nj, :], in_=p2[0:nj, 0:64], func=AF.Copy, r=[p2k], w=[ok])
                    A("act", "dma_start", out=SC["vcmp"][g][jt * 128:jt * 128 + nj, :], in_=o[0:nj, :], r=[ok], dma=ok)

def phase_cmp(cx, l, W, C, SC):
    A = cx.S.add
    ident = cx.sb([128, 128], BF16, "ident")
    A("sp", "dma_start", out=ident[:, :], in_=C["ident"][:, :], w=["ident"], dma="ident")
    cm = cx.sb([128, 5, 512], BF16, "cm")
    A("sp", "dma_start", out=cm[:, :, :], in_=C["cmask"].rearrange("m p c -> p m c"), w=["cm"], dma="cm")
    FB = cx.sb([128, NT, 64], F32, "FB")
    for q4 in range(4):
        A("act", "dma_start", out=FB[:, q4 * 8:(q4 + 1) * 8, :], in_=C["fbias"][q4 * 1024:(q4 + 1) * 1024, :].rearrange("(t p) n -> p t n", p=128), w=["FB"], dma="FB")
    G = cx.sb([128, NT, 24], F32, "G")
    for q4 in range(4):
        A("sp", "dma_start", out=G[:, q4 * 8:(q4 + 1) * 8, :], in_=SC["gate"][q4 * 1024:(q4 + 1) * 1024, :].rearrange("(t p) c -> p t c", p=128), w=["G"], dma="G")
    kc = []; va = []; ov = cx.sb([128, 2, 64], BF16, "ov")
    A("sp", "dma_start", out=ov[:, :, :], in_=C["overlap"].rearrange("(t p) n -> p t n", p=128), w=["ov"], dma="ov")
    for g in range(2):
        t = cx.sb([64, 256], BF16, "kcm"); A("sp", "dma_start", out=t[:, 0:255], in_=SC["kcmpT"][g][:, 0:255], w=["kcm%d" % g], dma="kcm%d" % g)
        kc.append(t)
        v = cx.sb([128, 2, 65], BF16, "vca")
        A("pool", "memset", ap=v[:, :, :], constant=1.0, w=["vca%d" % g])
        A("sp", "dma_start", out=v[:, 0, 0:64], in_=SC["vcmp"][g][0:128, :], r=["vca%d" % g], w=["vca%d" % g], dma="vca%d" % g)
        A("sp", "dma_start", out=v[0:127, 1, 0:64], in_=SC["vcmp"][g][128:255, :], r=["vca%d" % g], w=["vca%d" % g], dma="vca%d" % g)
        va.append(v)
    qr = cx.ring(4, [64, 512], BF16, "qn")
    sps = cx.psring(3, [128, 512], F32, "sps")
    ops_ = cx.psring(2, [128, 512], F32, "ops")
    ips = cx.psring(2, [128, 512], F32, "ips")
    tps = cx.psring(1, [128, 1024], BF16, "tps")
    pr = cx.ring(4, [128, 512], BF16, "pT")
    sm = cx.ring(4, [128, 12], F32, "sm")
    oc = cx.ring(2, [128, 4, 512], F32, "oc")
    acc = cx.ring(2, [128, 4, 64], F32, "acc")
    tI = cx.ring(2, [128, 4, 64], F32, "tI")
    mx = cx.ring(2, [128, 4, 16], F32, "mx")
    mr = cx.ring(8, [128, 64], F32, "mr")
    nmr = cx.ring(2, [128, 4, 64], BF16, "nm")
    nto = cx.ring(2, [64, 512], BF16, "nto")
    ctxs = [dict(j=j, g=g, h4=h4, hh=g * 4 + h4) for j in range(NQ) for g in range(2) for h4 in range(4)]
    jst = {}; gst = {}

    def emit_qk(ci):
        cxx = ctxs[ci]; j, g, hh = cxx["j"], cxx["g"], cxx["hh"]
        kts = [(0, 128)] + ([(1, 127)] if j >= 4 else [])
        q, qk = qr.next()
        A("sp", "dma_start", out=q[:, :], in_=SC["qnT"][hh][:, j * 512:(j + 1) * 512], w=[qk], dma=qk)
        pts = []
        for (kt, nk) in kts:
            sp, sk = sps.next()
            midx = j if kt == 0 else j - 4
            partial = (kt == 0 and j <= 4) or kt == 1
            A("pe", "matmul", out=sp[0:nk, :], lhsT=kc[g][:, kt * 128:kt * 128 + nk], rhs=q[:, :], start=True, stop=not partial,
              r=["kcm%d" % g, qk], w=[sk])
            if partial:
                A("pe", "matmul", out=sp[0:nk, :], lhsT=ident[0:nk, 0:nk], rhs=cm[0:nk, midx, :], start=False, stop=True,
                  r=["ident", "cm"], w=[sk])
            p_, pk = pr.next()
            A("act", "activation", out=p_[0:nk, :], in_=sp[0:nk, :], func=AF.Exp, scale=0.125, r=[sk], w=[pk])
            pts.append((p_, pk, kt, nk))
        cxx["pts"] = pts

    pending = []

    def process(ci):
        cxx = ctxs[ci]; j, g, h4, hh = cxx["j"], cxx["g"], cxx["h4"], cxx["hh"]
        if j not in jst: jst[j] = oc.next()
        oct_, ock = jst[j]
        if (j, g) not in gst: gst[(j, g)] = acc.next()
        at, ak = gst[(j, g)]
        pts = cxx["pts"]
        op, opk = ops_.next(); ip, ipk = ips.next()
        op = op[:, 0:260].rearrange("p (q c) -> p q c", c=65); ip = ip[:, 0:256].rearrange("p (q c) -> p q c", c=64)
        for qs in range(4):
            for i, (p_, pk, kt, nk) in enumerate(pts):
                A("pe", "matmul", out=op[:, qs, :], lhsT=p_[0:nk, qs * 128:(qs + 1) * 128], rhs=va[g][0:nk, kt, :],
                  start=(i == 0), stop=(i == len(pts) - 1), r=[pk, "vca%d" % g], w=[opk])
        for qs in range(4):
            for i, (p_, pk, kt, nk) in enumerate(pts):
                A("pe", "matmul", out=ip[:, qs, :], lhsT=p_[0:nk, qs * 128:(qs + 1) * 128], rhs=ov[0:nk, kt, :],
                  start=(i == 0), stop=(i == len(pts) - 1), r=[pk, "ov"], w=[ipk])
        s, sk_ = sm.next()
        A("dve", "tensor_scalar", out=s[:, 0:4], in0=op[:, :, 64], scalar1=1e-30, scalar2=None, op0=ALU.max, r=[opk], w=[(sk_, 0)])
        A("dve", "reciprocal", out=s[:, 4:8], in_=s[:, 0:4], r=[(sk_, 0)], w=[(sk_, 1)])
        A("dve", "tensor_tensor", out=s[:, 8:12], in0=s[:, 4:8], in1=G[:, 4 * j:4 * j + 4, 3 * hh], op=ALU.mult, r=[(sk_, 1), "G"], w=[(sk_, 2)])
        A("dve", "tensor_tensor", out=oct_[:, :, hh * 64:(hh + 1) * 64], in0=op[:, :, 0:64],
          in1=s[:, 8:12].unsqueeze(2).to_broadcast([128, 4, 64]), op=ALU.mult, r=[opk, (sk_, 2)], w=[(ock, hh)])
        if h4 == 0:
            A("dve", "tensor_tensor", out=at[:, :, :], in0=ip[:, :, :], in1=s[:, 4:8].unsqueeze(2).to_broadcast([128, 4, 64]),
              op=ALU.mult, r=[ipk, (sk_, 1)], w=[ak])
        else:
            ti, tik = tI.next()
            A("dve", "tensor_tensor", out=ti[:, :, :], in0=ip[:, :, :], in1=s[:, 4:8].unsqueeze(2).to_broadcast([128, 4, 64]),
              op=ALU.mult, r=[ipk, (sk_, 1)], w=[tik])
            A("dve", "tensor_tensor", out=at[:, :, :], in0=at[:, :, :], in1=ti[:, :, :], op=ALU.add, r=[ak, tik], w=[ak])
        if h4 == 3:
            A("dve", "tensor_tensor", out=at[:, :, :], in0=at[:, :, :], in1=FB[:, 4 * j:4 * j + 4, :], op=ALU.add, r=[ak, "FB"], w=[ak])
            m, mk = mx.next(); nm, nmk = nmr.next()
            rs_ = [mr.next() for _ in range(4)]
            for qs in range(4):
                A("dve", "max", out=m[:, qs, 0:8], in_=at[:, qs, :], r=[ak], w=[(mk, qs, 0)])
            for qs in range(4):
                A("dve", "match_replace", out=rs_[qs][0][:, :], in_to_replace=m[:, qs, 0:8], in_values=at[:, qs, :], imm_value=-3.0e4,
                  r=[ak, (mk, qs, 0)], w=[rs_[qs][1]])
            for qs in range(4):
                A("dve", "max", out=m[:, qs, 8:16], in_=rs_[qs][0][:, :], r=[rs_[qs][1]], w=[(mk, qs, 1)])
            for qs in range(4):
                A("dve", "tensor_scalar", out=nm[:, qs, :], in0=at[:, qs, :], scalar1=m[:, qs, 15:16], scalar2=1.0,
                  op0=ALU.is_ge, op1=ALU.subtract, r=[ak, (mk, qs, 1)], w=[(nmk, qs)])

            def pe_part():
                tp, tpk = tps.next()
                for qs in range(4):
                    A("pe", "transpose", out=tp[0:64, qs * 128:(qs + 1) * 128], in_=nm[:, qs, :], identity=ident[:, :], r=[(nmk, qs), "ident"], w=[tpk])
                no, nok = nto.next()
                A("act", "activation", out=no[:, :], in_=tp[0:64, 0:512], func=AF.Copy, r=[tpk], w=[nok])
                A("pool", "dma_start", out=SC["negT"][g][:, j * 512:(j + 1) * 512], in_=no[:, :], r=[nok], dma=nok)
            pending.append((ci + 2, pe_part))
        if hh == 7:
            A("pool", "dma_start", out=SC["ocmp"][j * 512:(j + 1) * 512, :].rearrange("(q p) c -> p q c", p=128), in_=oct_[:, :, :],
              r=[(ock, hh_) for hh_ in range(8)], dma=ock)

    emit_qk(0)
    for ci in range(len(ctxs)):
        if ci == 2: load_attn_consts(cx, C, SC)
        if ci + 1 < len(ctxs): emit_qk(ci + 1)
        process(ci)
        while pending and pending[0][0] <= ci:
            pending.pop(0)[1]()
    for _, fn in pending:
        fn()


def phase_attn(cx, l, W, C, SC):
    A = cx.S.add
    ident, am, G, ks, kw, Vall = cx.attn_pre
    zl = cx.sb([1, 128], BF16, "zl"); zr = cx.sb([1, 512], BF16, "zr")
    A("pool", "memset", ap=zl[:, :], constant=0.0, w=["zl"])
    A("pool", "memset", ap=zr[:, :], constant=0.0, w=["zr"])
    qn = cx.ring(3, [128, 512], BF16, "QN")
    sps = cx.psring(3, [128, 512], F32, "sps")
    osr = cx.psring(2, [128, 512], F32, "os")
    owr = cx.psring(2, [128, 512], F32, "ow")
    tps = cx.psring(1, [128, 1024], BF16, "tps")
    pr = cx.ring(6, [128, 512], BF16, "pT")
    sm = cx.ring(4, [128, 24], F32, "sm")
    ocr = cx.ring(2, [128, 4, 512], F32, "oc")
    nzr = cx.ring(2, [128, 4, 512], BF16, "nz")
    t1r = cx.ring(2, [128, 4, 64], F32, "t1"); t2r = cx.ring(2, [128, 4, 64], F32, "t2")
    obr = cx.ring(2, [128, 4, 64], BF16, "ob")
    otr = cx.ring(2, [64, 512], BF16, "ot")
    ctxs = []
    for j in range(NQ):
        for hh in range(8):
            ctxs.append(dict(j=j, hh=hh, g=hh // 4))
    blocks = []
    for ci, cxx in enumerate(ctxs):
        j = cxx["j"]
        for br in range(2):
            kbs = list(range(0, 4 * j + 4)) if br == 0 else list(range(max(0, 4 * j - 4), 4 * j + 4))
            for kb in kbs:
                if kb >= 4 * j: mi = kb - 4 * j
                elif br == 1: mi = 4 + kb - (4 * j - 4)
                else: mi = None
                blocks.append(dict(ci=ci, br=br, kb=kb, mi=mi, first=(kb == kbs[0]), last=(kb == kbs[-1])))
    jstate = {}

    def load_ctx(ci):
        cxx = ctxs[ci]
        if "q" in cxx: return
        j, hh, g = cxx["j"], cxx["hh"], cxx["g"]
        if j not in jstate:
            oc, ock = ocr.next(); nz, nzk = nzr.next()
            A("sp", "dma_start", out=oc[:, :, :], in_=SC["ocmp"][j * 512:(j + 1) * 512, :].rearrange("(q p) c -> p q c", p=128), w=[ock], dma=ock)
            A("sp", "dma_start", out=nz[:, :, :], in_=SC["nzs"][j * 512:(j + 1) * 512, :].rearrange("(q p) c -> p q c", p=128), w=[nzk], dma=nzk)
            jstate[j] = (oc, ock, nz, nzk)
        q, qk = qn.next()
        A("sp", "dma_start", out=q[0:64, :], in_=SC["qrT"][hh][:, j * 512:(j + 1) * 512], w=[qk], dma=qk)
        A("sp", "dma_start", out=q[64:128, :], in_=SC["negT"][g][:, j * 512:(j + 1) * 512], w=[qk], dma=qk)
        cxx["q"] = (q, qk)

    def emit_qk(n):
        b = blocks[n]; cxx = ctxs[b["ci"]]
        load_ctx(b["ci"])
        if b["ci"] + 1 < len(ctxs): load_ctx(b["ci"] + 1)
        q, qk = cxx["q"]; g = cxx["g"]; kb = b["kb"]; mi = b["mi"]
        sp, sk = sps.next()
        c0, c1 = (0, 512) if mi is None else ((128 * mi, 512) if mi < 4 else (0, 128 * (mi - 3)))
        b["cr"] = (c0, c1)
        if b["br"] == 0:
            A("pe", "matmul", out=sp[:, c0:c1], lhsT=ks[g][:, kb * 128:(kb + 1) * 128], rhs=q[:, c0:c1], start=True, stop=True,
              r=["ks%d" % g, qk], w=[sk])
        else:
            A("pe", "matmul", out=sp[:, c0:c1], lhsT=kw[g][:, kb * 128:(kb + 1) * 128], rhs=q[0:64, c0:c1], start=True, stop=True,
              r=["kw%d" % g, qk], w=[sk])
        if mi is not None:
            qsp = mi if mi < 4 else mi - 4
            A("pe", "matmul", out=sp[:, qsp * 128:(qsp + 1) * 128], lhsT=ident[:, :], rhs=am[:, mi, qsp * 128:(qsp + 1) * 128],
              start=False, stop=True, skip_group_check=True, r=["ident", "am"], w=[sk])
        b["sp"] = (sp, sk)

    pending = []

    def epilogue(cxx):
        j, hh = cxx["j"], cxx["hh"]
        oc, ock, nz, nzk = jstate[j]
        (os_, osk), (ow_, owk) = cxx["o"]
        s, sk_ = sm.next()
        A("dve", "tensor_scalar", out=s[:, 0:4], in0=os_[:, :, 64], scalar1=1e-30, scalar2=None, op0=ALU.max, r=[osk], w=[(sk_, 0)])
        A("dve", "tensor_scalar", out=s[:, 4:8], in0=ow_[:, :, 64], scalar1=1e-30, scalar2=None, op0=ALU.max, r=[owk], w=[(sk_, 0)])
        A("dve", "reciprocal", out=s[:, 8:16], in_=s[:, 0:8], r=[(sk_, 0)], w=[(sk_, 1)])
        A("dve", "tensor_tensor", out=s[:, 16:20], in0=s[:, 8:12], in1=G[:, 4 * j:4 * j + 4, 3 * hh + 1], op=ALU.mult, r=[(sk_, 1), "G"], w=[(sk_, 2)])
        A("dve", "tensor_tensor", out=s[:, 20:24], in0=s[:, 12:16], in1=G[:, 4 * j:4 * j + 4, 3 * hh + 2], op=ALU.mult, r=[(sk_, 1), "G"], w=[(sk_, 2)])
        t1, t1k = t1r.next(); t2, t2k = t2r.next(); ob, obk2 = obr.next()
        A("dve", "tensor_tensor", out=t1[:, :, :], in0=os_[:, :, 0:64], in1=s[:, 16:20].unsqueeze(2).to_broadcast([128, 4, 64]), op=ALU.mult,
          r=[osk, (sk_, 2)], w=[t1k])
        A("dve", "tensor_tensor", out=t2[:, :, :], in0=ow_[:, :, 0:64], in1=s[:, 20:24].unsqueeze(2).to_broadcast([128, 4, 64]), op=ALU.mult,
          r=[owk, (sk_, 2)], w=[t2k])
        A("dve", "tensor_tensor", out=t1[:, :, :], in0=t1[:, :, :], in1=t2[:, :, :], op=ALU.add, r=[t1k, t2k], w=[t1k])
        A("dve", "tensor_tensor", out=t1[:, :, :], in0=t1[:, :, :], in1=oc[:, :, hh * 64:(hh + 1) * 64], op=ALU.add, r=[t1k, ock], w=[t1k])
        A("dve", "tensor_tensor", out=ob[:, :, :], in0=t1[:, :, :], in1=nz[:, :, hh * 64:(hh + 1) * 64], op=ALU.mult, r=[t1k, nzk], w=[obk2])

        def pe_part():
            tp, tpk = tps.next()
            for qs in range(4):
                A("pe", "transpose", out=tp[0:64, qs * 128:(qs + 1) * 128], in_=ob[:, qs, :], identity=ident[:, :], r=[obk2, "ident"], w=[tpk])
            ot, otk = otr.next()
            A("dve", "tensor_copy", out=ot[:, :], in_=tp[0:64, 0:512], r=[tpk], w=[otk])
            A("pool", "dma_start", out=SC["onsaT"][hh * 64:(hh + 1) * 64, j * 512:(j + 1) * 512], in_=ot[:, :], r=[otk], dma=otk)
        return pe_part

    LOOK = 2
    for n in range(min(LOOK, len(blocks))):
        emit_qk(n)
    for n, b in enumerate(blocks):
        if n == 48:
            prefetch_out_weights(cx, l, W)
        if n + LOOK < len(blocks):
            emit_qk(n + LOOK)
        cxx = ctxs[b["ci"]]; g = cxx["g"]; kb = b["kb"]; mi = b["mi"]; br = b["br"]
        if b["first"]:
            ob_, obk = (osr if br == 0 else owr).next()
            o = ob_[:, 0:260].rearrange("p (q c) -> p q c", c=65)
            A("pe", "matmul", out=ob_[:, 0:260], lhsT=zl[:, :], rhs=zr[:, 0:260], start=True, stop=False, r=["zl", "zr"], w=[obk])
            cxx.setdefault("o", []).append((o, obk))
        o, obk = cxx["o"][br]
        sp, sk = b["sp"]
        p_, pk = pr.next()
        c0, c1 = b["cr"]
        A("act", "activation", out=p_[:, c0:c1], in_=sp[:, c0:c1], func=AF.Exp, scale=0.125, r=[sk], w=[pk])
        vc0 = (br * 2 + g) * 65
        for qs in range(4):
            if mi is not None and mi < 4 and qs < mi: continue
            if mi is not None and mi >= 4 and qs > mi - 4: continue
            A("pe", "matmul", out=o[:, qs, :], lhsT=p_[:, qs * 128:(qs + 1) * 128], rhs=Vall[:, kb, vc0:vc0 + 65], start=False,
              stop=(b["last"] and qs == 3), r=[pk, "Vall"], w=[obk])
        if b["last"] and br == 1:
            pending.append((n + 8, epilogue(cxx)))
        while pending and pending[0][0] <= n:
            pending.pop(0)[1]()
    for _, fn in pending:
        fn()


def alloc_attn_consts(cx):
    sb = cx.asb
    ident = sb([128, 128], BF16, "identA")
    am = sb([128, 8, 512], BF16, "am")
    G = sb([128, NT, 24], F32, "GA")
    ks = [sb([128, S], BF16, "ksa") for _ in range(2)]
    kw = [sb([64, S], BF16, "kwa") for _ in range(2)]
    Vall = sb([128, NT, 260], BF16, "Vall")
    cx.attn_pre = (ident, am, G, ks, kw, Vall)


def load_attn_consts(cx, C, SC):
    A = cx.S.add
    ident, am, G, ks, kw, Vall = cx.attn_pre
    A("act", "dma_start", out=ident[:, :], in_=C["ident"][:, :], w=["identA"], dma="identA")
    for g in range(2):
        A("act", "dma_start", out=ks[g][0:64, :], in_=SC["ksT"][g], w=["ks%d" % g], dma="ks%d" % g)
        A("act", "dma_start", out=ks[g][64:128, :], in_=C["emat"][:, :], w=["ks%d" % g], dma="ks%d" % g)
        A("act", "dma_start", out=kw[g][:, :], in_=SC["kwT"][g], w=["kw%d" % g], dma="kw%d" % g)
    for q4 in range(4):
        A("act", "dma_start", out=Vall[:, q4 * 8:(q4 + 1) * 8, :], in_=SC["vsw"][q4 * 1024:(q4 + 1) * 1024, :].rearrange("(t p) c -> p t c", p=128),
          w=["Vall"], dma="Vall")
    A("act", "dma_start", out=am[:, :, :], in_=C["amask"].rearrange("m p c -> p m c"), w=["am"], dma="am")
    for q4 in range(4):
        A("act", "dma_start", out=G[:, q4 * 8:(q4 + 1) * 8, :], in_=SC["gate"][q4 * 1024:(q4 + 1) * 1024, :].rearrange("(t p) c -> p t c", p=128), w=["GA"], dma="GA")


def phase_gla_a(cx, l, W, C, SC):
    A = cx.S.add
    ident = cx.sb([128, 128], BF16, "ident")
    A("sp", "dma_start", out=ident[:, :], in_=C["ident"][:, :], w=["ident"], dma="ident")
    one_t = cx.sb([128, 1], F32, "one"); eps_t = cx.sb([128, 1], F32, "eps")
    A("pool", "memset", ap=one_t[:, :], constant=1.0, w=["one"])
    A("pool", "memset", ap=eps_t[:, :], constant=EPS, w=["eps"])
    rst = cx.sb([128, 512], F32, "rst")
    A("sp", "dma_start", out=rst[:, :], in_=C["rst"][:, :], w=["rst"], dma="rst")
    tri = cx.sb([128, 64], BF16, "tri")
    A("sp", "dma_start", out=tri[:, :], in_=C["tri"][:, :], w=["tri"], dma="tri")
    gnb = cx.sb([128, 128], F32, "gnb")
    A("sp", "dma_start", out=gnb[:, :], in_=W["gla_g_norm"][l:l + 1, :].partition_broadcast(128)[:, 0, :], w=["gnb"], dma="gnb")
    waf = cx.sb([16, 256], F32, "waf"); wab = cx.sb([16, 256], BF16, "wab")
    A("sp", "dma_start", out=waf[:, :], in_=W["gla_w_a"][l], w=["waf"], dma="waf")
    A("pool", "tensor_copy", out=wab[:, :], in_=waf[:, :], r=["waf"], w=["wab"])
    nb = cx.sb([128, 2], F32, "nb")
    for b in range(2):
        A("sp", "dma_start", out=nb[:, b:b + 1], in_=W["gla_b_a"][l][b * 128:(b + 1) * 128, :], w=["nbf"], dma="nbf")
    A("pool", "tensor_scalar", out=nb[:, :], in0=nb[:, :], scalar1=-1.0, scalar2=None, op0=ALU.mult, r=["nbf"], w=["nb"])
    ga = cx.sb([16, S], BF16, "ga")
    A("sp", "dma_start", out=ga[:, :], in_=SC["gaT"][:, :], w=["ga"], dma="ga")
    qg, kg, dec, gv, kgT = cx.gla
    for q4 in range(4):
        A("sp", "dma_start", out=gv[:, q4 * 8:(q4 + 1) * 8, :], in_=SC["gv"][q4 * 1024:(q4 + 1) * 1024, :].rearrange("(t p) c -> p t c", p=128), w=["gv"], dma="gv")
    ps = cx.psring(3, [128, 512], F32, "gps")
    tps = cx.psring(3, [128, 1024], BF16, "tps")
    inr = cx.ring(8, [128, 512], BF16, "gin")
    f1 = cx.ring(2, [128, 512], F32, "f1"); f2 = cx.ring(3, [128, 512], F32, "f2"); f3 = cx.ring(3, [128, 512], F32, "f3")
    f4 = cx.ring(3, [128, 512], F32, "f4"); f5 = cx.ring(3, [128, 512], F32, "f5")
    tiles = [(b, j) for j in range(NQ) for b in range(2)]
    TT = [dict() for _ in tiles]

    def g0(i):
        b, j = tiles[i]; d = TT[i]
        js = slice(j * 512, (j + 1) * 512)
        pt, pk = ps.next()
        A("pe", "matmul", out=pt[:, :], lhsT=wab[:, b * 128:(b + 1) * 128], rhs=ga[:, js], start=True, stop=True, r=["wab", "ga"], w=[pk])
        e1, e1k = f1.next(); d["sp"] = f2.next(); sp_, spk = d["sp"]
        A("act", "activation", out=e1[:, :], in_=pt[:, :], func=AF.Exp, scale=-1.0, bias=nb[:, b:b + 1], r=[pk, "nb"], w=[e1k])
        A("act", "activation", out=sp_[:, :], in_=e1[:, :], func=AF.Ln, scale=1.0, bias=one_t[:, 0:1], r=[e1k, "one"], w=[spk])
        d["qi"] = inr.next(); d["ki"] = inr.next()
        A("sp", "dma_start", out=d["qi"][0][:, :], in_=SC["gqT"][b * 128:(b + 1) * 128, js], w=[d["qi"][1]], dma=d["qi"][1])
        A("sp", "dma_start", out=d["ki"][0][:, :], in_=SC["gkT"][b * 128:(b + 1) * 128, js], w=[d["ki"][1]], dma=d["ki"][1])

    def g1(i):
        d = TT[i]; sp_, spk = d["sp"]
        d["c"] = f3.next(); c_, ck = d["c"]
        A("dve", "tensor_tensor_scan", out=c_[:, :], data0=rst[:, :], data1=sp_[:, :], initial=0.0, op0=ALU.mult, op1=ALU.add,
          r=["rst", spk], w=[ck])

    def g2(i):
        d = TT[i]; c_, ck = d["c"]
        d["E1"] = f4.next(); d["E2"] = f5.next()
        A("act", "activation", out=d["E1"][0][:, :], in_=c_[:, :], func=AF.Exp, scale=-1.0 / 16, r=[ck], w=[d["E1"][1]])
        A("act", "activation", out=d["E2"][0][:, :], in_=c_[:, :], func=AF.Exp, scale=1.0 / 16, r=[ck], w=[d["E2"][1]])

    def g3(i):
        b, j = tiles[i]; d = TT[i]
        js = slice(j * 512, (j + 1) * 512)
        (E1, E1k), (E2, E2k), (qi, qik), (ki, kik) = d["E1"], d["E2"], d["qi"], d["ki"]
        A("dve", "tensor_tensor", out=qg[b][:, js], in0=qi[:, :], in1=E1[:, :], op=ALU.mult, r=[qik, E1k], w=[("qg", b, j)])
        A("dve", "tensor_tensor", out=kg[b][:, js], in0=ki[:, :], in1=E2[:, :], op=ALU.mult, r=[kik, E2k], w=[("kg", b, j)])
        A("pool", "tensor_copy", out=dec[b][:, 8 * j:8 * j + 8], in_=E1[:, 63:512:64], r=[E1k], w=[("dec", b, j)])

    def g4(i):
        b, j = tiles[i]
        if b != 1: return
        for t in range(4 * j, 4 * j + 4):
            tp, tpk = tps.next()
            for bb in range(2):
                A("pe", "transpose", out=tp[:, bb * 128:(bb + 1) * 128], in_=kg[bb][:, t * 128:(t + 1) * 128], identity=ident[:, :],
                  r=[("kg", bb, j), "ident"], w=[tpk])
            A("act", "activation", out=kgT[:, t, :], in_=tp[:, 0:256], func=AF.Copy, r=[tpk], w=[("kgT", t)])
        TT[i].clear(); TT[i - 1].clear()

    stages = (g0, g1, g2, g3, None, g4)
    for step in range(len(tiles) + len(stages) - 1):
        for si, fn in enumerate(stages):
            i = step - si
            if fn is not None and 0 <= i < len(tiles): fn(i)

def phase_gla_b(cx, l, W, C, SC):
    A = cx.S.add
    qg, kg, dec, gv, kgT = cx.gla
    ident = cx.sb([128, 128], BF16, "ident")
    A("sp", "dma_start", out=ident[:, :], in_=C["ident"][:, :], w=["ident"], dma="ident")
    eps_t = cx.sb([128, 1], F32, "eps")
    A("pool", "memset", ap=eps_t[:, :], constant=EPS, w=["eps"])
    tri = cx.sb([128, 64], BF16, "tri")
    A("sp", "dma_start", out=tri[:, :], in_=C["tri"][:, :], w=["tri"], dma="tri")
    gnb = cx.sb([128, 128], F32, "gnb")
    A("sp", "dma_start", out=gnb[:, :], in_=W["gla_g_norm"][l:l + 1, :].partition_broadcast(128)[:, 0, :], w=["gnb"], dma="gnb")
    ps = cx.psring(1, [128, 512], F32, "gps")
    tps = cx.psring(1, [128, 1024], BF16, "tps")
    ps.i = 0
    pA, pAk = ps.next()
    atb = [pA, cx.ps([128, 512], F32, "atb")]; atk = [pAk, "ps:at1"]
    ub = [cx.psring(2, [128, 512], F32, "ub%d" % cp) for cp in range(2)]
    attb = cx.sb([128, NT, 4, 64], BF16, "attb")
    S_all = [cx.sb([128, 2 * NT, 128], BF16, "Sall%d" % b) for b in range(2)]
    R = [cx.sb([128, 128], F32, "R") for _ in range(2)]
    for b in range(2):
        A("pool", "memset", ap=S_all[b][:, 0, :], constant=0.0, w=[("S", b, 0)])
    for t in range(NT):
        for par in range(2):
            hp = par * 64
            for cp in range(2):
                tok = slice(t * 128 + cp * 64, t * 128 + cp * 64 + 64)
                for b in range(2):
                    A("pe", "matmul", out=atb[par][cp * 64:(cp + 1) * 64, b * 64:(b + 1) * 64], lhsT=kg[b][hp:hp + 64, tok], rhs=qg[b][hp:hp + 64, tok],
                      start=True, stop=True, r=[("kg", b, t // 4), ("qg", b, t // 4)], w=[atk[par]])
        ubt = []
        for cp in range(2):
            u, uk = ub[cp].next(); ubt.append((u, uk))
            for h in range(4):
                b, hp = h // 2, (h % 2) * 64
                A("pe", "matmul", out=u[hp:hp + 64, b * 128:(b + 1) * 128], lhsT=kgT[cp * 64:(cp + 1) * 64, t, h * 64:(h + 1) * 64],
                  rhs=gv[cp * 64:(cp + 1) * 64, t, h * 128:(h + 1) * 128], start=True, stop=True, r=[("kgT", t), "gv"], w=[uk])
        for par in range(2):
            A("dve", "tensor_tensor", out=attb[:, t, :, :].rearrange("p (b q) c -> p b q c", q=2)[:, :, par, :],
              in0=atb[par][:, 0:128].rearrange("p (b c) -> p b c", c=64),
              in1=tri[:, :].unsqueeze(1).to_broadcast([128, 2, 64]), op=ALU.mult, r=[atk[par], "tri"], w=[("att", t, par)])
        for cp in range(2):
            c = 2 * t + cp
            u, uk = ubt[cp]
            for b in range(2):
                U = u[:, b * 128:(b + 1) * 128]
                if c == 0:
                    A("dve", "tensor_copy", out=R[b][:, :], in_=U, r=[uk], w=[("R", b)])
                else:
                    A("dve", "scalar_tensor_tensor", out=R[b][:, :], in0=R[b][:, :], scalar=dec[b][:, c - 1:c], in1=U, op0=ALU.mult, op1=ALU.add,
                      r=[("R", b), ("dec", b, (c - 1) // 8), uk], w=[("R", b)])
                if c < 2 * NT - 1:
                    A("pool", "tensor_scalar", out=S_all[b][:, c + 1, :], in0=R[b][:, :], scalar1=dec[b][:, c:c + 1], scalar2=1.0, op0=ALU.mult, op1=ALU.mult,
                      r=[("R", b), ("dec", b, c // 8)], w=[("S", b, c + 1)])
    oir = [(atb[0], atk[0]), (atb[1], atk[1])]
    oer = [[ub[par].tiles[i] for par in range(2)] for i in range(2)]
    oekn = [[(ub[par].name, i) for par in range(2)] for i in range(2)]
    grr = cx.ring(3, [128, 512], BF16, "gr"); gnr = cx.ring(4, [128, 4, 128], F32, "gn")
    ssr = cx.ring(3, [128, 12], F32, "ss"); junk = cx.sb([128, 128], BF16, "junk")
    oesr = cx.ring(2, [128, 512], F32, "oe"); osr = cx.ring(4, [128, 512], F32, "osum")
    ogr = cx.ring(3, [128, 512], BF16, "og"); otr = cx.ring(2, [128, 4, 128], BF16, "ogT")
    T = [dict() for _ in range(NT)]
    junkr = cx.ring(4, [128, 128], BF16, "junk4")

    def s0(t):
        d = T[t]
        d["oi"] = oir[t % 2]; d["oeb"] = oer[t % 2]; d["oek"] = oekn[t % 2]
        oib, oik = d["oi"]; oeb = d["oeb"]; oek = d["oek"]
        d["gr"] = grr.next()
        gr, grk = d["gr"]
        A("sp", "dma_start", out=gr[:, :], in_=SC["grs"][t * 128:(t + 1) * 128, :], w=[grk], dma=grk)
        for cp in range(2):
            for h in range(4):
                b, par = h // 2, h % 2
                A("pe", "matmul", out=oib[cp * 64:(cp + 1) * 64, h * 128:(h + 1) * 128], lhsT=attb[cp * 64:(cp + 1) * 64, t, h, :],
                  rhs=gv[cp * 64:(cp + 1) * 64, t, h * 128:(h + 1) * 128], start=True, stop=True, r=[("att", t, par), "gv"], w=[oik])
        for par in range(2):
            hp = par * 64
            for cp in range(2):
                c = 2 * t + cp
                tok = slice(t * 128 + cp * 64, t * 128 + cp * 64 + 64)
                for b in range(2):
                    A("pe", "matmul", out=oeb[par][cp * 64:(cp + 1) * 64, b * 128:(b + 1) * 128], lhsT=qg[b][hp:hp + 64, tok], rhs=S_all[b][hp:hp + 64, c, :],
                      start=True, stop=True, r=[("qg", b, t // 4), ("S", b, c)], w=[oek[par]])

    def s1(t):
        d = T[t]
        oib, oik = d["oi"]; oeb = d["oeb"]; oek = d["oek"]; gr, grk = d["gr"]
        oe, oekk = oesr.next(); d["osum"] = osr.next(); d["gn"] = gnr.next()
        osum, osk = d["osum"]; gn, gnk = d["gn"]
        for par in range(2):
            A("act", "activation", out=oe[:, :].rearrange("p (b q c) -> p b q c", q=2, c=128)[:, :, par, :],
              in_=oeb[par][:, 0:256].rearrange("p (b c) -> p b c", c=128), func=AF.Copy, r=[oek[par]], w=[(oekk, par)])
        A("dve", "tensor_tensor", out=osum[:, :], in0=oib[:, :], in1=oe[:, :], op=ALU.add, r=[oik, (oekk, 0), (oekk, 1)], w=[osk])
        A("pool", "tensor_tensor", out=gn[:, :, :], in0=gr[:, :].rearrange("p (h c) -> p h c", c=128),
          in1=gnb[:, :].unsqueeze(1).to_broadcast([128, 4, 128]), op=ALU.mult, r=[grk, "gnb"], w=[gnk])

    def s2(t):
        d = T[t]
        osum, osk = d["osum"]
        d["ss"] = ssr.next(); ss, ssk = d["ss"]
        for h in range(4):
            jt, jk = junkr.next()
            A("act", "activation", out=jt[:, :], in_=osum[:, h * 128:(h + 1) * 128], func=AF.Square, accum_out=ss[:, h:h + 1],
              r=[osk], w=[jk, (ssk, 0, h)])
        A("act", "activation", out=ss[:, 4:8], in_=ss[:, 0:4], func=AF.Ln, scale=1.0 / 128, bias=eps_t[:, 0:1],
          r=[(ssk, 0, h) for h in range(4)] + ["eps"], w=[(ssk, 1)])
        A("act", "activation", out=ss[:, 8:12], in_=ss[:, 4:8], func=AF.Exp, scale=-0.5, r=[(ssk, 1)], w=[(ssk, 2)])

    def s3(t):
        d = T[t]
        osum, osk = d["osum"]; gn, gnk = d["gn"]; ss, ssk = d["ss"]
        d["og"] = ogr.next(); og, ogk = d["og"]
        for h in range(4):
            A("dve", "scalar_tensor_tensor", out=og[:, h * 128:(h + 1) * 128], in0=osum[:, h * 128:(h + 1) * 128], scalar=ss[:, 8 + h:9 + h],
              in1=gn[:, h, :], op0=ALU.mult, op1=ALU.mult, r=[osk, (ssk, 2), gnk], w=[ogk])

    def s4(t):
        og, ogk = T[t]["og"]
        tp, tpk = tps.next()
        for fc in range(4):
            A("pe", "transpose", out=tp[:, fc * 128:(fc + 1) * 128], in_=og[:, fc * 128:(fc + 1) * 128], identity=ident[:, :], r=[ogk, "ident"], w=[tpk])
        ot, otk = otr.next()
        A("act", "activation", out=ot[:, :, :], in_=tp[:, 0:512].rearrange("p (f c) -> p f c", c=128), func=AF.Copy, r=[tpk], w=[otk])
        A("act", "dma_start", out=SC["oglaT"][:, t * 128:(t + 1) * 128].rearrange("(f p) c -> p f c", p=128), in_=ot[:, :, :], r=[otk], dma=otk)
        T[t].clear()

    stages = (s0, s1, s2, s3, s4)
    for step in range(NT + len(stages) - 1):
        for si, fn in enumerate(stages):
            t = step - si
            if 0 <= t < NT: fn(t)

def alloc_gla(cx):
    qg = [cx.lsb([128, S], BF16, "qg") for _ in range(2)]
    kg = [cx.lsb([128, S], BF16, "kg") for _ in range(2)]
    dec = [cx.lsb([128, 64], F32, "dec") for _ in range(2)]
    gv = cx.lsb([128, NT, 512], BF16, "gv")
    kgT = cx.lsb([128, NT, 256], BF16, "kgTM")
    cx.gla = (qg, kg, dec, gv, kgT)


def alloc_out_weights(cx):
    cx.wout = (cx.lsb([128, 4, 1024], BF16, "wun"), cx.lsb([128, 4, 1024], BF16, "wug"), cx.lsb([128, 8, 1024], BF16, "wo"))


def phase_out(cx, l, xsrc, dst, W, C, SC):
    A = cx.S.add
    eps_t = cx.sb([128, 1], F32, "eps")
    A("pool", "memset", ap=eps_t[:, :], constant=EPS, w=["eps"])
    gbc = cx.sb([128, D], F32, "gbc")
    A("sp", "dma_start", out=gbc[:, :], in_=W["g_post"][l:l + 1, :].partition_broadcast(128)[:, 0, :], w=["gbc"], dma="gbc")
    wun, wug, wo = cx.wout
    inr = cx.ring(2, [128, 4, 512], BF16, "on"); igr = cx.ring(2, [128, 4, 512], BF16, "og")
    mr = cx.ring(8, [128, 512], BF16, "mg")
    ups = cx.psring(4, [128, 512], F32, "ups")
    ops_ = cx.psring(4, [128, 512], F32, "ops")
    t1r = cx.ring(3, [128, 512], F32, "t1"); t2r = cx.ring(3, [128, 512], F32, "t2")
    yT = cx.ring(2, [128, 8, 512], BF16, "yT")
    ssr = cx.ring(2, [128, 8], F32, "ss"); junk = cx.sb([128, 512], BF16, "junk")
    xr = cx.ring(2, [128, D], F32, "x"); rr = cx.ring(2, [128, D], F32, "r")
    ys = {}

    def emit_up(j):
        js = slice(j * 512, (j + 1) * 512)
        on, onk = inr.next(); og, ogk = igr.next(); y, yk = yT.next()
        A("sp", "dma_start", out=on[:, :, :], in_=SC["onsaT"][:, js].rearrange("(k p) c -> p k c", p=128), w=[onk], dma=onk)
        A("sp", "dma_start", out=og[:, :, :], in_=SC["oglaT"][:, js].rearrange("(k p) c -> p k c", p=128), w=[ogk], dma=ogk)
        for cb in range(8):
            un, unk = ups.next(); ug, ugk = ups.next()
            for kc in range(4):
                A("pe", "matmul", out=un[:, :], lhsT=wun[:, kc, cb * 128:(cb + 1) * 128], rhs=on[:, kc, :], start=(kc == 0), stop=(kc == 3),
                  r=["w_up_nsa", onk], w=[unk])
            for kc in range(4):
                A("pe", "matmul", out=ug[:, :], lhsT=wug[:, kc, cb * 128:(cb + 1) * 128], rhs=og[:, kc, :], start=(kc == 0), stop=(kc == 3),
                  r=["w_up_gla", ogk], w=[ugk])
            mn, mnk = mr.next(); mg, mgk = mr.next()
            A("sp", "dma_start", out=mn[:, :], in_=SC["mgT"][cb * 128:(cb + 1) * 128, js], w=[mnk], dma=mnk)
            A("sp", "dma_start", out=mg[:, :], in_=SC["mgT"][1024 + cb * 128:1024 + (cb + 1) * 128, js], w=[mgk], dma=mgk)
            t1, t1k = t1r.next(); t2, t2k = t2r.next()
            A("dve", "tensor_tensor", out=t1[:, :], in0=un[:, :], in1=mn[:, :], op=ALU.mult, r=[unk, mnk], w=[t1k])
            A("dve", "tensor_tensor", out=t2[:, :], in0=ug[:, :], in1=mg[:, :], op=ALU.mult, r=[ugk, mgk], w=[t2k])
            A("pool", "tensor_tensor", out=y[:, cb, :], in0=t1[:, :], in1=t2[:, :], op=ALU.add, r=[t1k, t2k], w=[(yk, cb)])
        ys[j] = (y, yk)

    def emit_out(j):
        y, yk = ys[j]
        for qs in range(4):
            t = 4 * j + qs
            ss, ssk = ssr.next(); xt, xk = xr.next(); rt, rk = rr.next()
            A("sp", "dma_start", out=xt[:, :], in_=xsrc[t * 128:(t + 1) * 128, :], w=[xk], dma=xk)
            ob = []
            for half in range(2):
                o, ok = ops_.next()
                for cb in range(8):
                    A("pe", "matmul", out=o[:, :], lhsT=y[:, cb, qs * 128:(qs + 1) * 128], rhs=wo[:, cb, half * 512:(half + 1) * 512],
                      start=(cb == 0), stop=(cb == 7), r=[(yk, cb), "w_out"], w=[ok])
                A("act", "activation", out=junk[:, :], in_=o[:, :], func=AF.Square, accum_out=ss[:, half:half + 1], r=[ok], w=["junk", (ssk, half)])
                ob.append((o, ok))
            A("dve", "tensor_tensor", out=ss[:, 2:3], in0=ss[:, 0:1], in1=ss[:, 1:2], op=ALU.add, r=[(ssk, 0), (ssk, 1)], w=[(ssk, 2)])
            A("act", "activation", out=ss[:, 3:4], in_=ss[:, 2:3], func=AF.Ln, scale=1.0 / D, bias=eps_t[:, 0:1], r=[(ssk, 2), "eps"], w=[(ssk, 3)])
            A("act", "activation", out=ss[:, 4:5], in_=ss[:, 3:4], func=AF.Exp, scale=-0.5, r=[(ssk, 3)], w=[(ssk, 4)])
            for half, (o, ok) in enumerate(ob):
                hs = slice(half * 512, (half + 1) * 512)
                A("dve", "scalar_tensor_tensor", out=rt[:, hs], in0=o[:, :], scalar=ss[:, 4:5], in1=gbc[:, hs], op0=ALU.mult, op1=ALU.mult,
                  r=[ok, (ssk, 4), "gbc"], w=[(rk, half)])
            A("pool", "tensor_tensor", out=rt[:, :], in0=rt[:, :], in1=xt[:, :], op=ALU.add, r=[(rk, 0), (rk, 1), xk], w=[(rk, 0), (rk, 1)])
            A("pool", "dma_start", out=dst[t * 128:(t + 1) * 128, :], in_=rt[:, :], r=[(rk, 0), (rk, 1)], dma=rk)

    emit_up(0)
    for j in range(NQ):
        if j + 1 < NQ: emit_up(j + 1)
        emit_out(j)


def prefetch_out_weights(cx, l, W):
    A = cx.S.add
    wun, wug, wo = cx.wout
    stg = cx.ring(2, [128, 4, 1024], F32, "wstg")
    for (wt, nm, k0) in ((wun, "w_up_nsa", 0), (wug, "w_up_gla", 0), (wo, "w_out", 0), (wo, "w_out", 4)):
        s, sk = stg.next()
        A("act", "dma_start", out=s[:, :, :], in_=W[nm][l][k0 * 128:(k0 + 4) * 128, :].rearrange("(k p) c -> p k c", p=128), w=[sk], dma=sk)
        A("pool", "tensor_copy", out=wt[:, k0:k0 + 4, :], in_=s[:, :, :], r=[sk], w=[(nm, k0)])


WSPEC = [("g_pre", [DEPTH, D]), ("w_in", [DEPTH, D, DIN]),
         ("cmp_pos_k", [DEPTH, 64, 32]), ("cmp_w1_k", [DEPTH, 64, 32, 128]), ("cmp_w2_k", [DEPTH, 128, 64]),
         ("cmp_pos_v", [DEPTH, 64, 32]), ("cmp_w1_v", [DEPTH, 64, 32, 128]), ("cmp_w2_v", [DEPTH, 128, 64]),
         ("gla_w_a", [DEPTH, 16, 256]), ("gla_b_a", [DEPTH, 256, 1]), ("gla_g_norm", [DEPTH, 128]),
         ("w_up_nsa", [DEPTH, 512, D]), ("w_up_gla", [DEPTH, 512, D]), ("w_out", [DEPTH, D, D]), ("g_post", [DEPTH, D])]
SCSPEC = [("qrT", [8, 64, S], BF16), ("qnT", [8, 64, S], BF16), ("ksT", [2, 64, S], BF16), ("kwT", [2, 64, S], BF16),
          ("kcT", [2, 64, S], BF16), ("vcT", [2, 64, S], BF16), ("gqT", [256, S], BF16), ("gkT", [256, S], BF16),
          ("gaT", [16, S], BF16), ("mgT", [2048, S], BF16), ("vsw", [S, 260], BF16), ("gate", [S, 24], F32),
          ("nzs", [S, 512], BF16), ("gv", [S, 512], BF16), ("grs", [S, 512], BF16),
          ("kcmpT", [2, 64, 256], BF16), ("vcmp", [2, 256, 64], BF16), ("negT", [2, 64, S], BF16),
          ("ocmp", [S, 512], F32), ("onsaT", [512, S], BF16), ("oglaT", [512, S], BF16), ("resid", [S, D], F32)]


def make_consts():
    import ml_dtypes
    bf = ml_dtypes.bfloat16
    c = {}
    pos = np.arange(S, dtype=np.float32)
    inv_freq = (10000.0 ** (-np.arange(0, 64, 2, dtype=np.float32) / 64)).astype(np.float32)
    ang = pos[:, None] * inv_freq[None, :]
    cos, sin = np.cos(ang).astype(np.float32), np.sin(ang).astype(np.float32)
    c["cosT"] = np.ascontiguousarray(np.concatenate([cos, cos, cos, cos], 1).T)
    c["sinS"] = np.ascontiguousarray(np.concatenate([-sin, sin, -sin, sin], 1).T)
    c["ident"] = np.eye(128, dtype=np.float32).astype(bf)
    jc = np.arange(128)[:, None]; tq = np.arange(512)[None, :]
    c["cmask"] = np.stack([np.where(tq - 16 * jc < 31 - 512 * m, -BIG, 0.0) for m in range(5)]).astype(np.float32).astype(bf)
    kk = np.arange(128)[:, None]
    c["amask"] = np.stack([np.where(128 * r + kk > tq, -BIG, 0.0) for r in range(4)] +
                          [np.where(128 * r + kk <= tq, -BIG, 0.0) for r in range(4)]).astype(np.float32).astype(bf)
    c["emat"] = np.where(np.arange(S)[None, :] // 64 == np.arange(64)[:, None], BIG, 0.0).astype(np.float32).astype(bf)
    cs = 16 * np.arange(256)[:, None]; bs = 64 * np.arange(64)[None, :]
    ovl = ((cs < bs + 64) & (cs + 32 > bs)).astype(np.float32); ovl[255] = 0
    c["overlap"] = ovl.astype(bf)
    t = np.arange(S)[:, None]; n = np.arange(64)[None, :]; cur = t // 64
    forced = (n == 0) | (n == cur) | (n == cur - 1)
    c["rst"] = np.tile((np.arange(512) % 64 != 0).astype(np.float32)[None, :], (128, 1))
    c["tri"] = (np.arange(128)[:, None] % 64 <= np.arange(64)[None, :]).astype(np.float32).astype(bf)
    c["fbias"] = np.where(forced, 1.0e4, np.where(n <= cur, 0.0, -1.0e4)).astype(np.float32)
    return c


CSPEC = [("cosT", [128, S], F32), ("sinS", [128, S], F32), ("ident", [128, 128], BF16), ("cmask", [5, 128, 512], BF16),
         ("amask", [8, 128, 512], BF16), ("emat", [64, S], BF16), ("overlap", [256, 64], BF16), ("fbias", [S, 64], F32),
         ("rst", [128, 512], F32), ("tri", [128, 64], BF16)]


def build(nc, debug=None):
    debug = debug or {}
    dump = set(debug.get("dump", ()))
    stop = debug.get("stop", None)
    x = nc.dram_tensor("x", [S, D], F32, kind="ExternalInput").ap()
    W = {n: nc.dram_tensor(n, shp, F32, kind="ExternalInput").ap() for n, shp in WSPEC}
    C = {n: nc.dram_tensor(n, shp, dt, kind="ExternalInput").ap() for n, shp, dt in CSPEC}
    y = nc.dram_tensor("y", [S, D], F32, kind="ExternalOutput").ap()
    SC = {n: nc.dram_tensor("sc_" + n, shp, dt, kind=("ExternalOutput" if n in dump else "Internal")).ap()
          for n, shp, dt in SCSPEC}
    with ExitStack() as es:
        cx = Ctx(nc, es)
        cx.dbg = debug

        def phase(name, fn, *a):
            with ExitStack() as pes:
                cx.pes = pes
                fn(cx, *a)
                cx.S.emit(name)
        for l in range(DEPTH):
          with ExitStack() as les:
            cx.les = les
            xsrc = x if l == 0 else SC["resid"]
            phase("proj%d" % l, phase_proj, l, xsrc, W, C, SC)
            if stop == "proj": break
            alloc_out_weights(cx)
            phase("cmpr%d" % l, phase_compress, l, W, C, SC)
            if stop == "cmpr": break
            with ExitStack() as aes:
                cx.aes = aes
                alloc_attn_consts(cx)
                phase("cmp%d" % l, phase_cmp, l, W, C, SC)
                if stop != "cmp":
                    phase("attn%d" % l, phase_attn, l, W, C, SC)
            if stop in ("cmp", "attn"): break
            alloc_gla(cx)
            phase("glaa%d" % l, phase_gla_a, l, W, C, SC)
            phase("glab%d" % l, phase_gla_b, l, W, C, SC)
            if stop == "gla": break
            phase("out%d" % l, phase_out, l, xsrc, (SC["resid"] if l < DEPTH - 1 else y), W, C, SC)
            if stop == "out": break
    return nc


def prep_inputs(inputs):
    f = lambda a: np.ascontiguousarray(np.asarray(a, dtype=np.float32))
    w = {k: f(v) for k, v in inputs.items() if k != "x"}
    w["cmp_pos_k"] = f(w["cmp_pos_k"].transpose(0, 2, 1))
    w["cmp_pos_v"] = f(w["cmp_pos_v"].transpose(0, 2, 1))
    w["cmp_w1_k"] = f(w["cmp_w1_k"].transpose(0, 2, 1, 3))
    w["cmp_w1_v"] = f(w["cmp_w1_v"].transpose(0, 2, 1, 3))
    w["gla_b_a"] = f(w["gla_b_a"][:, :, None])
    w.update(make_consts())
    xs = f(inputs["x"])
    return [dict(w, x=xs[b]) for b in range(xs.shape[0])]


def kernel(**inputs):
    in_maps = prep_inputs(inputs)
    nc = bass.Bass("TRN2", target_bir_lowering=False)
    build(nc)
    res = run_bass_kernel_spmd(nc, in_maps, core_ids=list(range(8)))
    return np.stack([np.asarray(r["y"], dtype=np.float32) for r in res.results], axis=0)
```

```python
import numpy as np
from contextlib import ExitStack
import concourse.bass as bass
import concourse.mybir as mybir
from concourse.bass_utils import run_bass_kernel_spmd

F32 = mybir.dt.float32
BF16 = mybir.dt.bfloat16
AF = mybir.ActivationFunctionType
ALU = mybir.AluOpType
AX = mybir.AxisListType

S = 4096
D = 1024
NT = S // 128
NQ = S // 512
DIN = 5416
DEPTH = 2
BIG = 30000.0
EPS = 1e-6
C_Q, C_KC, C_VC, C_KS, C_VS, C_KW, C_VW, C_NG, C_NZ, C_GQ, C_GK, C_GV, C_GA, C_GR, C_MG = (
    0, 512, 640, 768, 896, 1024, 1152, 1280, 1304, 1816, 2072, 2328, 2840, 2856, 3368)


class Op:
    __slots__ = ("eng", "fn", "deps", "signal", "sigval", "dma", "dmaval", "idx", "bar")

    def __init__(self, eng, fn):
        self.eng = eng; self.fn = fn; self.deps = []; self.signal = False; self.sigval = None
        self.dma = None; self.dmaval = None; self.bar = None


class Sched:
    ENGS = ("pe", "act", "dve", "pool", "sp")

    def __init__(self, nc, es):
        self.nc = nc
        self.esem = {e: es.enter_context(nc.semaphore("sem_" + e)) for e in ("pe", "act", "dve", "pool")}
        self.ecount = {e: 0 for e in self.esem}
        self.dsem_free = [es.enter_context(nc.semaphore("dsem%d" % i)) for i in range(60)]
        self.dsem_free_sw = [es.enter_context(nc.semaphore("qsem%d" % i)) for i in range(24)]
        self.dsem_cnt = {id(s): 0 for s in self.dsem_free + self.dsem_free_sw}
        self.dkey = {}
        self.waited = {e: {} for e in self.ENGS}
        self.reset_phase()

    def is_excl(self, k):
        if isinstance(k, tuple): k = k[0]
        return isinstance(k, str) and k.startswith("ps:")

    def reset_phase(self):
        self.ops = {e: [] for e in self.ENGS}
        self.last_w = {}
        self.readers = {}
        for k, s in self.dkey.items():
            (self.dsem_free_sw if k[1] else self.dsem_free).append(s)
        self.dkey = {}

    def add(self, eng, meth, r=(), w=(), dma=None, **kw):
        op = Op(eng, (meth, kw))
        raw = set(); other = set()
        for k in r:
            x = self.last_w.get(k)
            if x is not None: raw.add(x)
            if self.is_excl(k):
                for y in self.readers.get(k, ()):
                    if y.eng != eng: raw.add(y)
        for k in w:
            x = self.last_w.get(k)
            if x is not None: raw.add(x)
            for y in self.readers.get(k, ()): other.add(y)
        for k in w:
            self.last_w[k] = op; self.readers[k] = []
        for k in r:
            self.readers.setdefault(k, []).append(op)
        deps = []
        for d in raw | other:
            if d is op: continue
            if d.dma is None and d.eng == eng:
                if eng == "pe": continue
                if d not in raw: continue
            deps.append(d)
            if d.dma is None: d.signal = True
        op.deps = deps
        if dma is not None:
            dma = (dma, eng == "pool")
            s = self.dkey.get(dma)
            if s is None:
                s = (self.dsem_free_sw if dma[1] else self.dsem_free).pop(); self.dkey[dma] = s
            self.dsem_cnt[id(s)] += 16
            op.dma = s; op.dmaval = self.dsem_cnt[id(s)]
        self.ops[eng].append(op)
        return op

    def emit(self, name):
        nc = self.nc
        lasts = {}
        for e in self.esem:
            real = [o for o in self.ops[e] if o.dma is None]
            if real:
                real[-1].signal = True
                lasts[e] = real[-1]
        for e in self.esem:
            for o in self.ops[e]:
                if o.dma is None and o.signal:
                    self.ecount[e] += 1; o.sigval = self.ecount[e]
        dma_totals = [(s, self.dsem_cnt[id(s)]) for s in self.dkey.values()]
        bar = [(self.esem[e], lasts[e].sigval) for e in lasts] + dma_totals

        def run(ename, eng):
            wd = self.waited[ename]

            def wait(sem, val):
                if wd.get(id(sem), 0) >= val: return
                wd[id(sem)] = val
                eng.wait_ge(sem, val)
            for o in self.ops[ename]:
                for d in o.deps:
                    if d.dma is not None: wait(d.dma, d.dmaval)
                    else: wait(self.esem[d.eng], d.sigval)
                ins = getattr(eng, o.fn[0])(**o.fn[1])
                if o.dma is not None: ins.then_inc(o.dma, 16)
                elif o.signal: ins.then_inc(self.esem[ename], 1)
            for sem, val in bar:
                wait(sem, val)
        with nc.Block() as block:
            @block.tensor
            def _(e): run("pe", e)

            @block.scalar
            def _(e): run("act", e)

            @block.vector
            def _(e): run("dve", e)

            @block.gpsimd
            def _(e): run("pool", e)

            @block.sync
            def _(e): run("sp", e)
        self.reset_phase()


class Ring:
    def __init__(self, tiles, name):
        self.tiles = tiles; self.i = 0; self.name = name

    def next(self):
        t = self.tiles[self.i % len(self.tiles)]; k = (self.name, self.i % len(self.tiles)); self.i += 1
        return t, k


class Ctx:
    def __init__(self, nc, es):
        self.nc = nc; self.S = Sched(nc, es); self.uid = 0; self.pes = None

    def sb(self, shape, dt, name=None):
        self.uid += 1
        return self.pes.enter_context(self.nc.sbuf_tensor("%s_%d" % (name or "t", self.uid), list(shape), dt))

    def lsb(self, shape, dt, name=None):
        self.uid += 1
        return self.les.enter_context(self.nc.sbuf_tensor("%s_%d" % (name or "t", self.uid), list(shape), dt))

    def asb(self, shape, dt, name=None):
        self.uid += 1
        return self.aes.enter_context(self.nc.sbuf_tensor("%s_%d" % (name or "t", self.uid), list(shape), dt))

    def ps(self, shape, dt, name=None):
        self.uid += 1
        return self.pes.enter_context(self.nc.psum_tensor("%s_%d" % (name or "p", self.uid), list(shape), dt))

    def ring(self, n, shape, dt, name):
        return Ring([self.sb(shape, dt, name) for _ in range(n)], name + str(self.uid))

    def psring(self, n, shape, dt, name):
        return Ring([self.ps(shape, dt, name) for _ in range(n)], "ps:" + name + str(self.uid))


def phase_proj(cx, l, xsrc, W, C, SC):
    A = cx.S.add
    nc = cx.nc
    ident = cx.sb([128, 128], BF16, "ident")
    A("sp", "dma_start", out=ident[:, :], in_=C["ident"][:, :], w=["ident"], dma="ident")
    eps_t = cx.sb([128, 1], F32, "eps")
    A("pool", "memset", ap=eps_t[:, :], constant=EPS, w=["eps"])
    gbc = cx.sb([128, D], F32, "gbc")
    A("sp", "dma_start", out=gbc[:, :], in_=W["g_pre"][l:l + 1, :].partition_broadcast(128)[:, 0, :], w=["gbc"], dma="gbc")
    hT = cx.sb([128, 8, S], BF16, "hT")
    cosT = cx.sb([128, S], F32, "cosT")
    sinS = cx.sb([128, S], F32, "sinS")
    xr = cx.ring(3, [128, D], F32, "x")
    hr = cx.ring(2, [128, D], BF16, "h")
    sq = cx.sb([128, D], BF16, "sq")
    st = cx.ring(4, [128, 4], F32, "st")
    pbf = cx.psring(2, [128, 1024], BF16, "pbf")
    P1 = [dict() for _ in range(NT)]

    def p1_s0(t):
        d = P1[t]
        d["x"] = xr.next(); d["s"] = st.next()
        xt, xk = d["x"]; s, sk = d["s"]
        A("sp", "dma_start", out=xt[:, :], in_=xsrc[t * 128:(t + 1) * 128, :], w=[xk], dma=xk)
        A("act", "activation", out=sq[:, :], in_=xt[:, :], func=AF.Square, accum_out=s[:, 0:1],
          r=[xk], w=["sq", (sk, 0)])
        A("act", "activation", out=s[:, 1:2], in_=s[:, 0:1], func=AF.Ln, scale=1.0 / D, bias=eps_t[:, 0:1],
          r=[(sk, 0), "eps"], w=[(sk, 1)])
        A("act", "activation", out=s[:, 2:3], in_=s[:, 1:2], func=AF.Exp, scale=-0.5,
          r=[(sk, 1)], w=[(sk, 2)])

    def p1_s1(t):
        d = P1[t]
        xt, xk = d["x"]; s, sk = d["s"]
        d["h"] = hr.next(); ht, hk = d["h"]
        A("dve", "scalar_tensor_tensor", out=ht[:, :], in0=xt[:, :], scalar=s[:, 2:3], in1=gbc[:, :], op0=ALU.mult, op1=ALU.mult,
          r=[xk, (sk, 2), "gbc"], w=[hk])

    def p1_s2(t):
        d = P1[t]
        ht, hk = d["h"]
        d["p"] = pbf.next(); pt, pk = d["p"]
        for kc in range(8):
            A("pe", "transpose", out=pt[:, kc * 128:(kc + 1) * 128], in_=ht[:, kc * 128:(kc + 1) * 128], identity=ident[:, :],
              r=[hk, "ident"], w=[pk])

    def p1_s3(t):
        pt, pk = P1[t]["p"]
        A("act", "activation", out=hT[:, :, t * 128:(t + 1) * 128], in_=pt[:, :].rearrange("p (k c) -> p k c", k=8), func=AF.Copy,
          r=[pk], w=[("hT", t)])
        P1[t].clear()

    def emit_p1():
        stages = (p1_s0, p1_s1, p1_s2, p1_s3)
        for step in range(NT + len(stages) - 1):
            for si, fn in enumerate(stages):
                t = step - si
                if 0 <= t < NT: fn(t)
            if step == 1:
                A("sp", "dma_start", out=cosT[:, :], in_=C["cosT"][:, :], w=["cosT"], dma="cosT")
                A("sp", "dma_start", out=sinS[:, :], in_=C["sinS"][:, :], w=["sinS"], dma="sinS")
            if step == 4:
                G[1][0]()
    stg = cx.ring(2, [128, 8, 528], F32, "stg")
    wbf = cx.ring(4, [128, 8, 536], BF16, "wbf")
    ps = cx.psring(6, [128, 512], F32, "pj")
    ost = cx.ring(6, [128, 512], BF16, "ost")
    tmp = cx.ring(4, [128, 512], F32, "ropet")
    gst = cx.ring(2, [128, 24], F32, "gst")
    vst = cx.ring(2, [128, 260], BF16, "vst")
    for _ in range(2):
        vt_, vk_ = vst.next()
        A("pool", "memset", ap=vt_[:, :], constant=1.0, w=[vk_])
    w_in = W["w_in"]
    hT_all = [("hT", t) for t in range(NT)]

    def load_stage(c0, n):
        sg, sgk = stg.next()
        A("sp", "dma_start", out=sg[:, :, 0:n], in_=w_in[l, :, c0:c0 + n].rearrange("(k p) c -> p k c", p=128),
          w=[sgk], dma=sgk)
        return sg, sgk

    def cast(wt, wk, d0, sg, sgk, s0, n, first):
        A("pool", "tensor_copy", out=wt[:, :, d0:d0 + n], in_=sg[:, :, s0:s0 + n], r=[sgk], w=[wk])

    def cast_perm(wt, wk, d0, sg, sgk, s0, nh):
        for half in range(2):
            A("pool", "tensor_copy",
                out=wt[:, :, d0:d0 + nh * 64].rearrange("p k (h two c) -> p k h two c", two=2, c=32)[:, :, :, half, :],
                in_=sg[:, :, s0:s0 + nh * 64].rearrange("p k (h two c) -> p k h two c", two=2, c=32)[:, :, :, 1 - half, :],
              r=[sgk], w=[wk])

    def fm_mm(wt, wk, c0, M, j, pt, pk):
        for kc in range(8):
            A("pe", "matmul", out=pt[0:M, :], lhsT=wt[:, kc, c0:c0 + M], rhs=hT[:, kc, j * 512:(j + 1) * 512],
                                             start=(kc == 0), stop=(kc == 7),
              r=[wk] + hT_all[4 * j:4 * j + 4], w=[pk])

    def tm_mm(wt, wk, c0, n, t, pt, pk):
        for kc in range(8):
            A("pe", "matmul", out=pt[:, 0:n], lhsT=hT[:, kc, t * 128:(t + 1) * 128], rhs=wt[:, kc, c0:c0 + n],
                                             start=(kc == 0), stop=(kc == 7),
              r=[wk, hT_all[t]], w=[pk])

    def store(dst_ap, o, ok, M, n, eng="act"):
        A(eng, "dma_start", out=dst_ap, in_=o[0:M, 0:n], r=[ok], w=[], dma=ok)

    def fm_copy(wt, wk, c0, M, dst, scale=None, func=None):
        for j in range(NQ):
            pt, pk = ps.next(); o, ok = ost.next()
            fm_mm(wt, wk, c0, M, j, pt, pk)
            if False:
                pass
            else:
                A("act", "activation", out=o[0:M, :], in_=pt[0:M, :], func=func or AF.Copy,
                                                          scale=1.0 if scale is None else scale, r=[pk], w=[ok])
            store(dst[:, j * 512:(j + 1) * 512], o, ok, M, 512)

    def fm_rope(wt, wk, c0, cp0, dst_r, dst_n, js_=None):
        for j in (range(NQ) if js_ is None else js_):
            p1, k1 = ps.next(); p2, k2 = ps.next()
            fm_mm(wt[0], wk[0], c0, 128, j, p1, k1)
            fm_mm(wt[1], wk[1], cp0, 128, j, p2, k2)
            js = slice(j * 512, (j + 1) * 512)
            if dst_n is not None:
                o, ok = ost.next()
                A("act", "activation", out=o[:, :], in_=p1[:, :], func=AF.Copy, r=[k1], w=[ok])
                store(dst_n[:, js], o, ok, 128, 512, "act")
            t1, tk1 = tmp.next(); t2, tk2 = tmp.next(); o, ok = ost.next()
            A("dve", "tensor_tensor", out=t1[:, :], in0=p1[:, :], in1=cosT[:, js], op=ALU.mult,
              r=[k1, "cosT"], w=[tk1])
            A("dve", "tensor_tensor", out=t2[:, :], in0=p2[:, :], in1=sinS[:, js], op=ALU.mult,
              r=[k2, "sinS"], w=[tk2])
            A("pool", "tensor_tensor", out=o[:, :], in0=t1[:, :], in1=t2[:, :], op=ALU.add,
              r=[tk1, tk2], w=[ok])
            store(dst_r[:, js], o, ok, 128, 512, "pool")

    G = []
    st_ = {}

    def g1_prep():
        sg, sgk = load_stage(C_Q, 512)
        wq, wqk = wbf.next(); wp, wpk = wbf.next()
        cast(wq, wqk, 0, sg, sgk, 0, 512, True)
        cast_perm(wp, wpk, 0, sg, sgk, 0, 8)
        st_["g1"] = (wq, wqk, wp, wpk)

    def g1_comp_j(j):
        wq, wqk, wp, wpk = st_["g1"]
        for hp2 in range(4):
            fm_rope((wq, wp), (wqk, wpk), hp2 * 128, hp2 * 128, SC["qrT"][2 * hp2:2 * hp2 + 2].rearrange("h d s -> (h d) s"),
                    SC["qnT"][2 * hp2:2 * hp2 + 2].rearrange("h d s -> (h d) s"), js_=[j])
    G.append((g1_prep, lambda: [g1_comp_j(j) for j in range(NQ)]))

    def g2_prep():
        sga, sgak = load_stage(C_KC, 512)
        sgb, sgbk = load_stage(C_KW, 280)
        w1, w1k = wbf.next()
        cast(w1, w1k, 0, sga, sgak, 0, 384, True)
        cast_perm(w1, w1k, 384, sga, sgak, 256, 2)
        w2, w2k = wbf.next()
        cast(w2, w2k, 0, sgb, sgbk, 0, 128, True)
        cast_perm(w2, w2k, 128, sgb, sgbk, 0, 2)
        cast(w2, w2k, 256, sga, sgak, 384, 128, False)
        cast(w2, w2k, 384, sgb, sgbk, 128, 152, False)
        st_["g2"] = (w1, w1k, w2, w2k)

    def g2_comp():
        w1, w1k, w2, w2k = st_["g2"]
        fm_copy(w1, w1k, 0, 128, SC["kcT"].rearrange("g d s -> (g d) s"))
        fm_copy(w1, w1k, 128, 128, SC["vcT"].rearrange("g d s -> (g d) s"))
        fm_rope((w1, w1), (w1k, w1k), 256, 384, SC["ksT"].rearrange("g d s -> (g d) s"), None)
        fm_rope((w2, w2), (w2k, w2k), 0, 128, SC["kwT"].rearrange("g d s -> (g d) s"), None)
        for t in range(NT):
            pt, pk = ps.next(); gt, gk = gst.next()
            tm_mm(w2, w2k, 256, 280, t, pt, pk)
            vt, vk = vst.next()
            A("act", "activation", out=vt[:, :].rearrange("p (f c) -> p f c", c=65)[:, :, 0:64],
              in_=pt[:, 0:256].rearrange("p (f c) -> p f c", c=64), func=AF.Copy, r=[pk], w=[vk])
            A("act", "activation", out=gt[:, :], in_=pt[:, 256:280], func=AF.Sigmoid, r=[pk], w=[gk])
            A("act", "dma_start", out=SC["vsw"][t * 128:(t + 1) * 128, :], in_=vt[:, :], r=[vk], dma=vk)
            A("act", "dma_start", out=SC["gate"][t * 128:(t + 1) * 128, :], in_=gt[:, :], r=[gk], dma=gk)
    G.append((g2_prep, g2_comp))

    def mk_tm(c0, n, off, name, func):
        key = "tm" + name

        def prep():
            sg, sgk = load_stage(c0, n)
            wt, wk = wbf.next()
            cast(wt, wk, 0, sg, sgk, off, 512, True)
            st_[key] = (wt, wk, sg, sgk)

        def comp():
            wt, wk, sg, sgk = st_[key]
            if name == "grs":
                fm_copy(wt, wk, 512, 16, SC["gaT"])
            for t in range(NT):
                pt, pk = ps.next(); o, ok = ost.next()
                tm_mm(wt, wk, 0, 512, t, pt, pk)
                A("act", "activation", out=o[:, :], in_=pt[:, :], func=(func or AF.Copy), r=[pk], w=[ok])
                store(SC[name][t * 128:(t + 1) * 128, :], o, ok, 128, 512)
        if name == "grs":
            def prep2():
                prep()
                wt, wk, sg, sgk = st_[key]
                cast(wt, wk, 512, sg, sgk, 0, 16, True)
            return prep2, comp
        return prep, comp
    for args in ((C_NZ, 512, 0, "nzs", AF.Silu), (C_GV, 512, 0, "gv", None), (C_GA, 528, 16, "grs", AF.Silu)):
        G.append(mk_tm(*args))

    def g4_prep():
        sg, sgk = load_stage(C_GQ, 512)
        wt, wk = wbf.next()
        cast(wt, wk, 0, sg, sgk, 0, 512, True)
        st_["g4"] = (wt, wk)

    def g4_comp():
        wt, wk = st_["g4"]
        for b in range(2):
            fm_copy(wt, wk, b * 128, 128, SC["gqT"][b * 128:(b + 1) * 128, :], scale=0.125)
            fm_copy(wt, wk, 256 + b * 128, 128, SC["gkT"][b * 128:(b + 1) * 128, :])
    G.append((g4_prep, g4_comp))

    def mk_mg(q4):
        def prep():
            sg, sgk = load_stage(C_MG + q4 * 512, 512)
            wt, wk = wbf.next()
            cast(wt, wk, 0, sg, sgk, 0, 512, True)
            st_["mg%d" % q4] = (wt, wk)

        def comp():
            wt, wk = st_["mg%d" % q4]
            for b in range(4):
                r0 = q4 * 512 + b * 128
                fm_copy(wt, wk, b * 128, 128, SC["mgT"][r0:r0 + 128, :], func=AF.Sigmoid)
        return prep, comp
    for q4 in range(4):
        G.append(mk_mg(q4))

    G[0][0]()
    emit_p1()
    for n, (prep, comp) in enumerate(G):
        if n >= 1 and n + 1 < len(G): G[n + 1][0]()
        comp()


def phase_compress(cx, l, W, C, SC):
    A = cx.S.add
    ps = cx.psring(4, [128, 512], F32, "cps")
    specs = (("kcT", "cmp_pos_k", "cmp_w1_k", "cmp_w2_k"), ("vcT", "cmp_pos_v", "cmp_w1_v", "cmp_w2_v"))
    T = []
    for kv, (src, posn, w1n, w2n) in enumerate(specs):
        d = {}
        w1f = cx.sb([64, 32, 128], F32, "w1f"); d["w1b"] = cx.sb([64, 32, 128], BF16, "w1b")
        pf = cx.sb([64, 32], F32, "pf"); d["pb"] = cx.sb([64, 32], BF16, "pb")
        w2f = cx.sb([128, 64], F32, "w2f"); d["w2b"] = cx.sb([128, 64], BF16, "w2b")
        d["cst"] = cx.sb([128, 1], F32, "cst")
        k = lambda s, kv=kv: "%s%d" % (s, kv)
        d["k"] = k
        A("sp", "dma_start", out=w1f[:, :, :], in_=W[w1n][l], w=[k("w1f")], dma=k("w1f"))
        A("sp", "dma_start", out=pf[:, :], in_=W[posn][l], w=[k("pf")], dma=k("pf"))
        A("sp", "dma_start", out=w2f[:, :], in_=W[w2n][l], w=[k("w2f")], dma=k("w2f"))
        d["kt"] = []
        for g in range(2):
            kt = cx.sb([64, S], BF16, "kt"); kk = "kt%d%d" % (kv, g)
            A("sp", "dma_start", out=kt[:, :], in_=SC[src][g], w=[kk], dma=kk)
            d["kt"].append((kt, kk))
        A("pool", "tensor_copy", out=d["w1b"][:, :, :], in_=w1f[:, :, :], r=[k("w1f")], w=[k("w1b")])
        A("pool", "tensor_copy", out=d["pb"][:, :], in_=pf[:, :], r=[k("pf")], w=[k("pb")])
        A("pool", "tensor_copy", out=d["w2b"][:, :], in_=w2f[:, :], r=[k("w2f")], w=[k("w2b")])
        T.append(d)
    for kv, d in enumerate(T):
        k = d["k"]; w1b = d["w1b"]; pb = d["pb"]; w2b = d["w2b"]; cst = d["cst"]
        pt, pk = ps.next()
        for li in range(32):
            A("pe", "matmul", out=pt[:, 0:1], lhsT=w1b[:, li, :], rhs=pb[:, li:li + 1], start=(li == 0), stop=(li == 31),
              r=[k("w1b"), k("pb")], w=[pk])
        A("dve", "tensor_copy", out=cst[:, :], in_=pt[:, 0:1], r=[pk], w=[k("cst")])
        for g in range(2):
            kt, kk = d["kt"][g]
            pt, pk = ps.next()
            for li in range(32):
                A("pe", "matmul", out=pt[:, 0:255], lhsT=w1b[:, li, :], rhs=kt[:, li:li + 16 * 254 + 1:16],
                  start=(li == 0), stop=(li == 31), r=[k("w1b"), kk], w=[pk])
            hb = cx.sb([128, 256], BF16, "hb"); hk = "hb%d%d" % (kv, g)
            A("act", "activation", out=hb[:, 0:255], in_=pt[:, 0:255], func=AF.Silu, bias=cst[:, 0:1], r=[pk, k("cst")], w=[hk])
            if kv == 0:
                p2, p2k = ps.next()
                A("pe", "matmul", out=p2[0:64, 0:255], lhsT=w2b[:, :], rhs=hb[:, 0:255], start=True, stop=True, r=[k("w2b"), hk], w=[p2k])
                o = cx.sb([64, 256], BF16, "kco"); ok = "kco%d" % g
                A("act", "activation", out=o[:, 0:255], in_=p2[0:64, 0:255], func=AF.Copy, r=[p2k], w=[ok])
                A("act", "dma_start", out=SC["kcmpT"][g][:, 0:255], in_=o[:, 0:255], r=[ok], dma=ok)
            else:
                for jt, nj in ((0, 128), (1, 127)):
                    p2, p2k = ps.next()
                    A("pe", "matmul", out=p2[0:nj, 0:64], lhsT=hb[:, jt * 128:jt * 128 + nj], rhs=w2b[:, :], start=True, stop=True,
                      r=[k("w2b"), hk], w=[p2k])
                    o = cx.sb([128, 64], BF16, "vco"); ok = "vco%d%d" % (g, jt)
                    A("act", "activation", out=o[0:nj, :], in_=p2[0:nj, 0:64], func=AF.Copy, r=[p2k], w=[ok])
                    A("act", "dma_start", out=SC["vcmp"][g][jt * 128:jt * 128 + nj, :], in_=o[0:nj, :], r=[ok], dma=ok)

def phase_cmp(cx, l, W, C, SC):
    A = cx.S.add
    ident = cx.sb([128, 128], BF16, "ident")
    A("sp", "dma_start", out=ident[:, :], in_=C["ident"][:, :], w=["ident"], dma="ident")
    cm = cx.sb([128, 5, 512], BF16, "cm")
    A("sp", "dma_start", out=cm[:, :, :], in_=C["cmask"].rearrange("m p c -> p m c"), w=["cm"], dma="cm")
    FB = cx.sb([128, NT, 64], F32, "FB")
    for q4 in range(4):
        A("act", "dma_start", out=FB[:, q4 * 8:(q4 + 1) * 8, :], in_=C["fbias"][q4 * 1024:(q4 + 1) * 1024, :].rearrange("(t p) n -> p t n", p=128), w=["FB"], dma="FB")
    G = cx.sb([128, NT, 24], F32, "G")
    for q4 in range(4):
        A("sp", "dma_start", out=G[:, q4 * 8:(q4 + 1) * 8, :], in_=SC["gate"][q4 * 1024:(q4 + 1) * 1024, :].rearrange("(t p) c -> p t c", p=128), w=["G"], dma="G")
    kc = []; va = []; ov = cx.sb([128, 2, 64], BF16, "ov")
    A("sp", "dma_start", out=ov[:, :, :], in_=C["overlap"].rearrange("(t p) n -> p t n", p=128), w=["ov"], dma="ov")
    for g in range(2):
        t = cx.sb([64, 256], BF16, "kcm"); A("sp", "dma_start", out=t[:, 0:255], in_=SC["kcmpT"][g][:, 0:255], w=["kcm%d" % g], dma="kcm%d" % g)
        kc.append(t)
        v = cx.sb([128, 2, 65], BF16, "vca")
        A("pool", "memset", ap=v[:, :, :], constant=1.0, w=["vca%d" % g])
        A("sp", "dma_start", out=v[:, 0, 0:64], in_=SC["vcmp"][g][0:128, :], r=["vca%d" % g], w=["vca%d" % g], dma="vca%d" % g)
        A("sp", "dma_start", out=v[0:127, 1, 0:64], in_=SC["vcmp"][g][128:255, :], r=["vca%d" % g], w=["vca%d" % g], dma="vca%d" % g)
        va.append(v)
    qr = cx.ring(4, [64, 512], BF16, "qn")
    sps = cx.psring(3, [128, 512], F32, "sps")
    ops_ = cx.psring(2, [128, 512], F32, "ops")
    ips = cx.psring(2, [128, 512], F32, "ips")
    tps = cx.psring(1, [128, 1024], BF16, "tps")
    pr = cx.ring(4, [128, 512], BF16, "pT")
    sm = cx.ring(4, [128, 12], F32, "sm")
    oc = cx.ring(2, [128, 4, 512], F32, "oc")
    acc = cx.ring(2, [128, 4, 64], F32, "acc")
    tI = cx.ring(2, [128, 4, 64], F32, "tI")
    mx = cx.ring(2, [128, 4, 16], F32, "mx")
    mr = cx.ring(8, [128, 64], F32, "mr")
    nmr = cx.ring(2, [128, 4, 64], BF16, "nm")
    nto = cx.ring(2, [64, 512], BF16, "nto")
    ctxs = [dict(j=j, g=g, h4=h4, hh=g * 4 + h4) for j in range(NQ) for g in range(2) for h4 in range(4)]
    jst = {}; gst = {}

    def emit_qk(ci):
        cxx = ctxs[ci]; j, g, hh = cxx["j"], cxx["g"], cxx["hh"]
        kts = [(0, 128)] + ([(1, 127)] if j >= 4 else [])
        q, qk = qr.next()
        A("sp", "dma_start", out=q[:, :], in_=SC["qnT"][hh][:, j * 512:(j + 1) * 512], w=[qk], dma=qk)
        pts = []
        for (kt, nk) in kts:
            sp, sk = sps.next()
            midx = j if kt == 0 else j - 4
            partial = (kt == 0 and j <= 4) or kt == 1
            A("pe", "matmul", out=sp[0:nk, :], lhsT=kc[g][:, kt * 128:kt * 128 + nk], rhs=q[:, :], start=True, stop=not partial,
              r=["kcm%d" % g, qk], w=[sk])
            if partial:
                A("pe", "matmul", out=sp[0:nk, :], lhsT=ident[0:nk, 0:nk], rhs=cm[0:nk, midx, :], start=False, stop=True,
                  r=["ident", "cm"], w=[sk])
            p_, pk = pr.next()
            A("act", "activation", out=p_[0:nk, :], in_=sp[0:nk, :], func=AF.Exp, scale=0.125, r=[sk], w=[pk])
            pts.append((p_, pk, kt, nk))
        cxx["pts"] = pts

    pending = []

    def process(ci):
        cxx = ctxs[ci]; j, g, h4, hh = cxx["j"], cxx["g"], cxx["h4"], cxx["hh"]
        if j not in jst: jst[j] = oc.next()
        oct_, ock = jst[j]
        if (j, g) not in gst: gst[(j, g)] = acc.next()
        at, ak = gst[(j, g)]
        pts = cxx["pts"]
        op, opk = ops_.next(); ip, ipk = ips.next()
        op = op[:, 0:260].rearrange("p (q c) -> p q c", c=65); ip = ip[:, 0:256].rearrange("p (q c) -> p q c", c=64)
        for qs in range(4):
            for i, (p_, pk, kt, nk) in enumerate(pts):
                A("pe", "matmul", out=op[:, qs, :], lhsT=p_[0:nk, qs * 128:(qs + 1) * 128], rhs=va[g][0:nk, kt, :],
                  start=(i == 0), stop=(i == len(pts) - 1), r=[pk, "vca%d" % g], w=[opk])
        for qs in range(4):
            for i, (p_, pk, kt, nk) in enumerate(pts):
                A("pe", "matmul", out=ip[:, qs, :], lhsT=p_[0:nk, qs * 128:(qs + 1) * 128], rhs=ov[0:nk, kt, :],
                  start=(i == 0), stop=(i == len(pts) - 1), r=[pk, "ov"], w=[ipk])
        s, sk_ = sm.next()
        A("dve", "tensor_scalar", out=s[:, 0:4], in0=op[:, :, 64], scalar1=1e-30, scalar2=None, op0=ALU.max, r=[opk], w=[(sk_, 0)])
        A("dve", "reciprocal", out=s[:, 4:8], in_=s[:, 0:4], r=[(sk_, 0)], w=[(sk_, 1)])
        A("dve", "tensor_tensor", out=s[:, 8:12], in0=s[:, 4:8], in1=G[:, 4 * j:4 * j + 4, 3 * hh], op=ALU.mult, r=[(sk_, 1), "G"], w=[(sk_, 2)])
        A("dve", "tensor_tensor", out=oct_[:, :, hh * 64:(hh + 1) * 64], in0=op[:, :, 0:64],
          in1=s[:, 8:12].unsqueeze(2).to_broadcast([128, 4, 64]), op=ALU.mult, r=[opk, (sk_, 2)], w=[(ock, hh)])
        if h4 == 0:
            A("dve", "tensor_tensor", out=at[:, :, :], in0=ip[:, :, :], in1=s[:, 4:8].unsqueeze(2).to_broadcast([128, 4, 64]),
              op=ALU.mult, r=[ipk, (sk_, 1)], w=[ak])
        else:
            ti, tik = tI.next()
            A("dve", "tensor_tensor", out=ti[:, :, :], in0=ip[:, :, :], in1=s[:, 4:8].unsqueeze(2).to_broadcast([128, 4, 64]),
              op=ALU.mult, r=[ipk, (sk_, 1)], w=[tik])
            A("dve", "tensor_tensor", out=at[:, :, :], in0=at[:, :, :], in1=ti[:, :, :], op=ALU.add, r=[ak, tik], w=[ak])
        if h4 == 3:
            A("dve", "tensor_tensor", out=at[:, :, :], in0=at[:, :, :], in1=FB[:, 4 * j:4 * j + 4, :], op=ALU.add, r=[ak, "FB"], w=[ak])
            m, mk = mx.next(); nm, nmk = nmr.next()
            rs_ = [mr.next() for _ in range(4)]
            for qs in range(4):
                A("dve", "max", out=m[:, qs, 0:8], in_=at[:, qs, :], r=[ak], w=[(mk, qs, 0)])
            for qs in range(4):
                A("dve", "match_replace", out=rs_[qs][0][:, :], in_to_replace=m[:, qs, 0:8], in_values=at[:, qs, :], imm_value=-3.0e4,
                  r=[ak, (mk, qs, 0)], w=[rs_[qs][1]])
            for qs in range(4):
                A("dve", "max", out=m[:, qs, 8:16], in_=rs_[qs][0][:, :], r=[rs_[qs][1]], w=[(mk, qs, 1)])
            for qs in range(4):
                A("dve", "tensor_scalar", out=nm[:, qs, :], in0=at[:, qs, :], scalar1=m[:, qs, 15:16], scalar2=1.0,
                  op0=ALU.is_ge, op1=ALU.subtract, r=[ak, (mk, qs, 1)], w=[(nmk, qs)])

            def pe_part():
                tp, tpk = tps.next()
                for qs in range(4):
                    A("pe", "transpose", out=tp[0:64, qs * 128:(qs + 1) * 128], in_=nm[:, qs, :], identity=ident[:, :], r=[(nmk, qs), "ident"], w=[tpk])
                no, nok = nto.next()
                A("act", "activation", out=no[:, :], in_=tp[0:64, 0:512], func=AF.Copy, r=[tpk], w=[nok])
                A("pool", "dma_start", out=SC["negT"][g][:, j * 512:(j + 1) * 512], in_=no[:, :], r=[nok], dma=nok)
            pending.append((ci + 2, pe_part))
        if hh == 7:
            A("pool", "dma_start", out=SC["ocmp"][j * 512:(j + 1) * 512, :].rearrange("(q p) c -> p q c", p=128), in_=oct_[:, :, :],
              r=[(ock, hh_) for hh_ in range(8)], dma=ock)

    emit_qk(0)
    for ci in range(len(ctxs)):
        if ci == 2: load_attn_consts(cx, C, SC)
        if ci + 1 < len(ctxs): emit_qk(ci + 1)
        process(ci)
        while pending and pending[0][0] <= ci:
            pending.pop(0)[1]()
    for _, fn in pending:
        fn()


def phase_attn(cx, l, W, C, SC):
    A = cx.S.add
    ident, am, G, ks, kw, Vall = cx.attn_pre
    zl = cx.sb([1, 128], BF16, "zl"); zr = cx.sb([1, 512], BF16, "zr")
    A("pool", "memset", ap=zl[:, :], constant=0.0, w=["zl"])
    A("pool", "memset", ap=zr[:, :], constant=0.0, w=["zr"])
    qn = cx.ring(3, [128, 512], BF16, "QN")
    sps = cx.psring(3, [128, 512], F32, "sps")
    osr = cx.psring(2, [128, 512], F32, "os")
    owr = cx.psring(2, [128, 512], F32, "ow")
    tps = cx.psring(1, [128, 1024], BF16, "tps")
    pr = cx.ring(6, [128, 512], BF16, "pT")
    sm = cx.ring(4, [128, 24], F32, "sm")
    ocr = cx.ring(2, [128, 4, 512], F32, "oc")
    nzr = cx.ring(2, [128, 4, 512], BF16, "nz")
    t1r = cx.ring(2, [128, 4, 64], F32, "t1"); t2r = cx.ring(2, [128, 4, 64], F32, "t2")
    obr = cx.ring(2, [128, 4, 64], BF16, "ob")
    otr = cx.ring(2, [64, 512], BF16, "ot")
    ctxs = []
    for j in range(NQ):
        for hh in range(8):
            ctxs.append(dict(j=j, hh=hh, g=hh // 4))
    blocks = []
    for ci, cxx in enumerate(ctxs):
        j = cxx["j"]
        for br in range(2):
            kbs = list(range(0, 4 * j + 4)) if br == 0 else list(range(max(0, 4 * j - 4), 4 * j + 4))
            for kb in kbs:
                if kb >= 4 * j: mi = kb - 4 * j
                elif br == 1: mi = 4 + kb - (4 * j - 4)
                else: mi = None
                blocks.append(dict(ci=ci, br=br, kb=kb, mi=mi, first=(kb == kbs[0]), last=(kb == kbs[-1])))
    jstate = {}

    def load_ctx(ci):
        cxx = ctxs[ci]
        if "q" in cxx: return
        j, hh, g = cxx["j"], cxx["hh"], cxx["g"]
        if j not in jstate:
            oc, ock = ocr.next(); nz, nzk = nzr.next()
            A("sp", "dma_start", out=oc[:, :, :], in_=SC["ocmp"][j * 512:(j + 1) * 512, :].rearrange("(q p) c -> p q c", p=128), w=[ock], dma=ock)
            A("sp", "dma_start", out=nz[:, :, :], in_=SC["nzs"][j * 512:(j + 1) * 512, :].rearrange("(q p) c -> p q c", p=128), w=[nzk], dma=nzk)
            jstate[j] = (oc, ock, nz, nzk)
        q, qk = qn.next()
        A("sp", "dma_start", out=q[0:64, :], in_=SC["qrT"][hh][:, j * 512:(j + 1) * 512], w=[qk], dma=qk)
        A("sp", "dma_start", out=q[64:128, :], in_=SC["negT"][g][:, j * 512:(j + 1) * 512], w=[qk], dma=qk)
        cxx["q"] = (q, qk)

    def emit_qk(n):
        b = blocks[n]; cxx = ctxs[b["ci"]]
        load_ctx(b["ci"])
        if b["ci"] + 1 < len(ctxs): load_ctx(b["ci"] + 1)
        q, qk = cxx["q"]; g = cxx["g"]; kb = b["kb"]; mi = b["mi"]
        sp, sk = sps.next()
        c0, c1 = (0, 512) if mi is None else ((128 * mi, 512) if mi < 4 else (0, 128 * (mi - 3)))
        b["cr"] = (c0, c1)
        if b["br"] == 0:
            A("pe", "matmul", out=sp[:, c0:c1], lhsT=ks[g][:, kb * 128:(kb + 1) * 128], rhs=q[:, c0:c1], start=True, stop=True,
              r=["ks%d" % g, qk], w=[sk])
        else:
            A("pe", "matmul", out=sp[:, c0:c1], lhsT=kw[g][:, kb * 128:(kb + 1) * 128], rhs=q[0:64, c0:c1], start=True, stop=True,
              r=["kw%d" % g, qk], w=[sk])
        if mi is not None:
            qsp = mi if mi < 4 else mi - 4
            A("pe", "matmul", out=sp[:, qsp * 128:(qsp + 1) * 128], lhsT=ident[:, :], rhs=am[:, mi, qsp * 128:(qsp + 1) * 128],
              start=False, stop=True, skip_group_check=True, r=["ident", "am"], w=[sk])
        b["sp"] = (sp, sk)

    pending = []

    def epilogue(cxx):
        j, hh = cxx["j"], cxx["hh"]
        oc, ock, nz, nzk = jstate[j]
        (os_, osk), (ow_, owk) = cxx["o"]
        s, sk_ = sm.next()
        A("dve", "tensor_scalar", out=s[:, 0:4], in0=os_[:, :, 64], scalar1=1e-30, scalar2=None, op0=ALU.max, r=[osk], w=[(sk_, 0)])
        A("dve", "tensor_scalar", out=s[:, 4:8], in0=ow_[:, :, 64], scalar1=1e-30, scalar2=None, op0=ALU.max, r=[owk], w=[(sk_, 0)])
        A("dve", "reciprocal", out=s[:, 8:16], in_=s[:, 0:8], r=[(sk_, 0)], w=[(sk_, 1)])
        A("dve", "tensor_tensor", out=s[:, 16:20], in0=s[:, 8:12], in1=G[:, 4 * j:4 * j + 4, 3 * hh + 1], op=ALU.mult, r=[(sk_, 1), "G"], w=[(sk_, 2)])
        A("dve", "tensor_tensor", out=s[:, 20:24], in0=s[:, 12:16], in1=G[:, 4 * j:4 * j + 4, 3 * hh + 2], op=ALU.mult, r=[(sk_, 1), "G"], w=[(sk_, 2)])
        t1, t1k = t1r.next(); t2, t2k = t2r.next(); ob, obk2 = obr.next()
        A("dve", "tensor_tensor", out=t1[:, :, :], in0=os_[:, :, 0:64], in1=s[:, 16:20].unsqueeze(2).to_broadcast([128, 4, 64]), op=ALU.mult,
          r=[osk, (sk_, 2)], w=[t1k])
        A("dve", "tensor_tensor", out=t2[:, :, :], in0=ow_[:, :, 0:64], in1=s[:, 20:24].unsqueeze(2).to_broadcast([128, 4, 64]), op=ALU.mult,
          r=[owk, (sk_, 2)], w=[t2k])
        A("dve", "tensor_tensor", out=t1[:, :, :], in0=t1[:, :, :], in1=t2[:, :, :], op=ALU.add, r=[t1k, t2k], w=[t1k])
        A("dve", "tensor_tensor", out=t1[:, :, :], in0=t1[:, :, :], in1=oc[:, :, hh * 64:(hh + 1) * 64], op=ALU.add, r=[t1k, ock], w=[t1k])
        A("dve", "tensor_tensor", out=ob[:, :, :], in0=t1[:, :, :], in1=nz[:, :, hh * 64:(hh + 1) * 64], op=ALU.mult, r=[t1k, nzk], w=[obk2])

        def pe_part():
            tp, tpk = tps.next()
            for qs in range(4):
                A("pe", "transpose", out=tp[0:64, qs * 128:(qs + 1) * 128], in_=ob[:, qs, :], identity=ident[:, :], r=[obk2, "ident"], w=[tpk])
            ot, otk = otr.next()
            A("dve", "tensor_copy", out=ot[:, :], in_=tp[0:64, 0:512], r=[tpk], w=[otk])
            A("pool", "dma_start", out=SC["onsaT"][hh * 64:(hh + 1) * 64, j * 512:(j + 1) * 512], in_=ot[:, :], r=[otk], dma=otk)
        return pe_part

    LOOK = 2
    for n in range(min(LOOK, len(blocks))):
        emit_qk(n)
    for n, b in enumerate(blocks):
        if n == 48:
            prefetch_out_weights(cx, l, W)
        if n + LOOK < len(blocks):
            emit_qk(n + LOOK)
        cxx = ctxs[b["ci"]]; g = cxx["g"]; kb = b["kb"]; mi = b["mi"]; br = b["br"]
        if b["first"]:
            ob_, obk = (osr if br == 0 else owr).next()
            o = ob_[:, 0:260].rearrange("p (q c) -> p q c", c=65)
            A("pe", "matmul", out=ob_[:, 0:260], lhsT=zl[:, :], rhs=zr[:, 0:260], start=True, stop=False, r=["zl", "zr"], w=[obk])
            cxx.setdefault("o", []).append((o, obk))
        o, obk = cxx["o"][br]
        sp, sk = b["sp"]
        p_, pk = pr.next()
        c0, c1 = b["cr"]
        A("act", "activation", out=p_[:, c0:c1], in_=sp[:, c0:c1], func=AF.Exp, scale=0.125, r=[sk], w=[pk])
        vc0 = (br * 2 + g) * 65
        for qs in range(4):
            if mi is not None and mi < 4 and qs < mi: continue
            if mi is not None and mi >= 4 and qs > mi - 4: continue
            A("pe", "matmul", out=o[:, qs, :], lhsT=p_[:, qs * 128:(qs + 1) * 128], rhs=Vall[:, kb, vc0:vc0 + 65], start=False,
              stop=(b["last"] and qs == 3), r=[pk, "Vall"], w=[obk])
        if b["last"] and br == 1:
            pending.append((n + 8, epilogue(cxx)))
        while pending and pending[0][0] <= n:
            pending.pop(0)[1]()
    for _, fn in pending:
        fn()


def alloc_attn_consts(cx):
    sb = cx.asb
    ident = sb([128, 128], BF16, "identA")
    am = sb([128, 8, 512], BF16, "am")
    G = sb([128, NT, 24], F32, "GA")
    ks = [sb([128, S], BF16, "ksa") for _ in range(2)]
    kw = [sb([64, S], BF16, "kwa") for _ in range(2)]
    Vall = sb([128, NT, 260], BF16, "Vall")
    cx.attn_pre = (ident, am, G, ks, kw, Vall)


def load_attn_consts(cx, C, SC):
    A = cx.S.add
    ident, am, G, ks, kw, Vall = cx.attn_pre
    A("act", "dma_start", out=ident[:, :], in_=C["ident"][:, :], w=["identA"], dma="identA")
    for g in range(2):
        A("act", "dma_start", out=ks[g][0:64, :], in_=SC["ksT"][g], w=["ks%d" % g], dma="ks%d" % g)
        A("act", "dma_start", out=ks[g][64:128, :], in_=C["emat"][:, :], w=["ks%d" % g], dma="ks%d" % g)
        A("act", "dma_start", out=kw[g][:, :], in_=SC["kwT"][g], w=["kw%d" % g], dma="kw%d" % g)
    for q4 in range(4):
        A("act", "dma_start", out=Vall[:, q4 * 8:(q4 + 1) * 8, :], in_=SC["vsw"][q4 * 1024:(q4 + 1) * 1024, :].rearrange("(t p) c -> p t c", p=128),
          w=["Vall"], dma="Vall")
    A("act", "dma_start", out=am[:, :, :], in_=C["amask"].rearrange("m p c -> p m c"), w=["am"], dma="am")
    for q4 in range(4):
        A("act", "dma_start", out=G[:, q4 * 8:(q4 + 1) * 8, :], in_=SC["gate"][q4 * 1024:(q4 + 1) * 1024, :].rearrange("(t p) c -> p t c", p=128), w=["GA"], dma="GA")


def phase_gla_a(cx, l, W, C, SC):
    A = cx.S.add
    ident = cx.sb([128, 128], BF16, "ident")
    A("sp", "dma_start", out=ident[:, :], in_=C["ident"][:, :], w=["ident"], dma="ident")
    one_t = cx.sb([128, 1], F32, "one"); eps_t = cx.sb([128, 1], F32, "eps")
    A("pool", "memset", ap=one_t[:, :], constant=1.0, w=["one"])
    A("pool", "memset", ap=eps_t[:, :], constant=EPS, w=["eps"])
    rst = cx.sb([128, 512], F32, "rst")
    A("sp", "dma_start", out=rst[:, :], in_=C["rst"][:, :], w=["rst"], dma="rst")
    tri = cx.sb([128, 64], BF16, "tri")
    A("sp", "dma_start", out=tri[:, :], in_=C["tri"][:, :], w=["tri"], dma="tri")
    gnb = cx.sb([128, 128], F32, "gnb")
    A("sp", "dma_start", out=gnb[:, :], in_=W["gla_g_norm"][l:l + 1, :].partition_broadcast(128)[:, 0, :], w=["gnb"], dma="gnb")
    waf = cx.sb([16, 256], F32, "waf"); wab = cx.sb([16, 256], BF16, "wab")
    A("sp", "dma_start", out=waf[:, :], in_=W["gla_w_a"][l], w=["waf"], dma="waf")
    A("pool", "tensor_copy", out=wab[:, :], in_=waf[:, :], r=["waf"], w=["wab"])
    nb = cx.sb([128, 2], F32, "nb")
    for b in range(2):
        A("sp", "dma_start", out=nb[:, b:b + 1], in_=W["gla_b_a"][l][b * 128:(b + 1) * 128, :], w=["nbf"], dma="nbf")
    A("pool", "tensor_scalar", out=nb[:, :], in0=nb[:, :], scalar1=-1.0, scalar2=None, op0=ALU.mult, r=["nbf"], w=["nb"])
    ga = cx.sb([16, S], BF16, "ga")
    A("sp", "dma_start", out=ga[:, :], in_=SC["gaT"][:, :], w=["ga"], dma="ga")
    qg, kg, dec, gv, kgT = cx.gla
    for q4 in range(4):
        A("sp", "dma_start", out=gv[:, q4 * 8:(q4 + 1) * 8, :], in_=SC["gv"][q4 * 1024:(q4 + 1) * 1024, :].rearrange("(t p) c -> p t c", p=128), w=["gv"], dma="gv")
    ps = cx.psring(3, [128, 512], F32, "gps")
    tps = cx.psring(3, [128, 1024], BF16, "tps")
    inr = cx.ring(8, [128, 512], BF16, "gin")
    f1 = cx.ring(2, [128, 512], F32, "f1"); f2 = cx.ring(3, [128, 512], F32, "f2"); f3 = cx.ring(3, [128, 512], F32, "f3")
    f4 = cx.ring(3, [128, 512], F32, "f4"); f5 = cx.ring(3, [128, 512], F32, "f5")
    tiles = [(b, j) for j in range(NQ) for b in range(2)]
    TT = [dict() for _ in tiles]

    def g0(i):
        b, j = tiles[i]; d = TT[i]
        js = slice(j * 512, (j + 1) * 512)
        pt, pk = ps.next()
        A("pe", "matmul", out=pt[:, :], lhsT=wab[:, b * 128:(b + 1) * 128], rhs=ga[:, js], start=True, stop=True, r=["wab", "ga"], w=[pk])
        e1, e1k = f1.next(); d["sp"] = f2.next(); sp_, spk = d["sp"]
        A("act", "activation", out=e1[:, :], in_=pt[:, :], func=AF.Exp, scale=-1.0, bias=nb[:, b:b + 1], r=[pk, "nb"], w=[e1k])
        A("act", "activation", out=sp_[:, :], in_=e1[:, :], func=AF.Ln, scale=1.0, bias=one_t[:, 0:1], r=[e1k, "one"], w=[spk])
        d["qi"] = inr.next(); d["ki"] = inr.next()
        A("sp", "dma_start", out=d["qi"][0][:, :], in_=SC["gqT"][b * 128:(b + 1) * 128, js], w=[d["qi"][1]], dma=d["qi"][1])
        A("sp", "dma_start", out=d["ki"][0][:, :], in_=SC["gkT"][b * 128:(b + 1) * 128, js], w=[d["ki"][1]], dma=d["ki"][1])

    def g1(i):
        d = TT[i]; sp_, spk = d["sp"]
        d["c"] = f3.next(); c_, ck = d["c"]
        A("dve", "tensor_tensor_scan", out=c_[:, :], data0=rst[:, :], data1=sp_[:, :], initial=0.0, op0=ALU.mult, op1=ALU.add,
          r=["rst", spk], w=[ck])

    def g2(i):
        d = TT[i]; c_, ck = d["c"]
        d["E1"] = f4.next(); d["E2"] = f5.next()
        A("act", "activation", out=d["E1"][0][:, :], in_=c_[:, :], func=AF.Exp, scale=-1.0 / 16, r=[ck], w=[d["E1"][1]])
        A("act", "activation", out=d["E2"][0][:, :], in_=c_[:, :], func=AF.Exp, scale=1.0 / 16, r=[ck], w=[d["E2"][1]])

    def g3(i):
        b, j = tiles[i]; d = TT[i]
        js = slice(j * 512, (j + 1) * 512)
        (E1, E1k), (E2, E2k), (qi, qik), (ki, kik) = d["E1"], d["E2"], d["qi"], d["ki"]
        A("dve", "tensor_tensor", out=qg[b][:, js], in0=qi[:, :], in1=E1[:, :], op=ALU.mult, r=[qik, E1k], w=[("qg", b, j)])
        A("dve", "tensor_tensor", out=kg[b][:, js], in0=ki[:, :], in1=E2[:, :], op=ALU.mult, r=[kik, E2k], w=[("kg", b, j)])
        A("pool", "tensor_copy", out=dec[b][:, 8 * j:8 * j + 8], in_=E1[:, 63:512:64], r=[E1k], w=[("dec", b, j)])

    def g4(i):
        b, j = tiles[i]
        if b != 1: return
        for t in range(4 * j, 4 * j + 4):
            tp, tpk = tps.next()
            for bb in range(2):
                A("pe", "transpose", out=tp[:, bb * 128:(bb + 1) * 128], in_=kg[bb][:, t * 128:(t + 1) * 128], identity=ident[:, :],
                  r=[("kg", bb, j), "ident"], w=[tpk])
            A("act", "activation", out=kgT[:, t, :], in_=tp[:, 0:256], func=AF.Copy, r=[tpk], w=[("kgT", t)])
        TT[i].clear(); TT[i - 1].clear()

    stages = (g0, g1, g2, g3, None, g4)
    for step in range(len(tiles) + len(stages) - 1):
        for si, fn in enumerate(stages):
            i = step - si
            if fn is not None and 0 <= i < len(tiles): fn(i)

def phase_gla_b(cx, l, W, C, SC):
    A = cx.S.add
    qg, kg, dec, gv, kgT = cx.gla
    ident = cx.sb([128, 128], BF16, "ident")
    A("sp", "dma_start", out=ident[:, :], in_=C["ident"][:, :], w=["ident"], dma="ident")
    eps_t = cx.sb([128, 1], F32, "eps")
    A("pool", "memset", ap=eps_t[:, :], constant=EPS, w=["eps"])
    tri = cx.sb([128, 64], BF16, "tri")
    A("sp", "dma_start", out=tri[:, :], in_=C["tri"][:, :], w=["tri"], dma="tri")
    gnb = cx.sb([128, 128], F32, "gnb")
    A("sp", "dma_start", out=gnb[:, :], in_=W["gla_g_norm"][l:l + 1, :].partition_broadcast(128)[:, 0, :], w=["gnb"], dma="gnb")
    ps = cx.psring(1, [128, 512], F32, "gps")
    tps = cx.psring(1, [128, 1024], BF16, "tps")
    ps.i = 0
    pA, pAk = ps.next()
    atb = [pA, cx.ps([128, 512], F32, "atb")]; atk = [pAk, "ps:at1"]
    ub = [cx.psring(2, [128, 512], F32, "ub%d" % cp) for cp in range(2)]
    attb = cx.sb([128, NT, 4, 64], BF16, "attb")
    S_all = [cx.sb([128, 2 * NT, 128], BF16, "Sall%d" % b) for b in range(2)]
    R = [cx.sb([128, 128], F32, "R") for _ in range(2)]
    for b in range(2):
        A("pool", "memset", ap=S_all[b][:, 0, :], constant=0.0, w=[("S", b, 0)])
    for t in range(NT):
        for par in range(2):
            hp = par * 64
            for cp in range(2):
                tok = slice(t * 128 + cp * 64, t * 128 + cp * 64 + 64)
                for b in range(2):
                    A("pe", "matmul", out=atb[par][cp * 64:(cp + 1) * 64, b * 64:(b + 1) * 64], lhsT=kg[b][hp:hp + 64, tok], rhs=qg[b][hp:hp + 64, tok],
                      start=True, stop=True, r=[("kg", b, t // 4), ("qg", b, t // 4)], w=[atk[par]])
        ubt = []
        for cp in range(2):
            u, uk = ub[cp].next(); ubt.append((u, uk))
            for h in range(4):
                b, hp = h // 2, (h % 2) * 64
                A("pe", "matmul", out=u[hp:hp + 64, b * 128:(b + 1) * 128], lhsT=kgT[cp * 64:(cp + 1) * 64, t, h * 64:(h + 1) * 64],
                  rhs=gv[cp * 64:(cp + 1) * 64, t, h * 128:(h + 1) * 128], start=True, stop=True, r=[("kgT", t), "gv"], w=[uk])
        for par in range(2):
            A("dve", "tensor_tensor", out=attb[:, t, :, :].rearrange("p (b q) c -> p b q c", q=2)[:, :, par, :],
              in0=atb[par][:, 0:128].rearrange("p (b c) -> p b c", c=64),
              in1=tri[:, :].unsqueeze(1).to_broadcast([128, 2, 64]), op=ALU.mult, r=[atk[par], "tri"], w=[("att", t, par)])
        for cp in range(2):
            c = 2 * t + cp
            u, uk = ubt[cp]
            for b in range(2):
                U = u[:, b * 128:(b + 1) * 128]
                if c == 0:
                    A("dve", "tensor_copy", out=R[b][:, :], in_=U, r=[uk], w=[("R", b)])
                else:
                    A("dve", "scalar_tensor_tensor", out=R[b][:, :], in0=R[b][:, :], scalar=dec[b][:, c - 1:c], in1=U, op0=ALU.mult, op1=ALU.add,
                      r=[("R", b), ("dec", b, (c - 1) // 8), uk], w=[("R", b)])
                if c < 2 * NT - 1:
                    A("pool", "tensor_scalar", out=S_all[b][:, c + 1, :], in0=R[b][:, :], scalar1=dec[b][:, c:c + 1], scalar2=1.0, op0=ALU.mult, op1=ALU.mult,
                      r=[("R", b), ("dec", b, c // 8)], w=[("S", b, c + 1)])
    oir = [(atb[0], atk[0]), (atb[1], atk[1])]
    oer = [[ub[par].tiles[i] for par in range(2)] for i in range(2)]
    oekn = [[(ub[par].name, i) for par in range(2)] for i in range(2)]
    grr = cx.ring(3, [128, 512], BF16, "gr"); gnr = cx.ring(4, [128, 4, 128], F32, "gn")
    ssr = cx.ring(3, [128, 12], F32, "ss"); junk = cx.sb([128, 128], BF16, "junk")
    oesr = cx.ring(2, [128, 512], F32, "oe"); osr = cx.ring(4, [128, 512], F32, "osum")
    ogr = cx.ring(3, [128, 512], BF16, "og"); otr = cx.ring(2, [128, 4, 128], BF16, "ogT")
    T = [dict() for _ in range(NT)]
    junkr = cx.ring(4, [128, 128], BF16, "junk4")

    def s0(t):
        d = T[t]
        d["oi"] = oir[t % 2]; d["oeb"] = oer[t % 2]; d["oek"] = oekn[t % 2]
        oib, oik = d["oi"]; oeb = d["oeb"]; oek = d["oek"]
        d["gr"] = grr.next()
        gr, grk = d["gr"]
        A("sp", "dma_start", out=gr[:, :], in_=SC["grs"][t * 128:(t + 1) * 128, :], w=[grk], dma=grk)
        for cp in range(2):
            for h in range(4):
                b, par = h // 2, h % 2
                A("pe", "matmul", out=oib[cp * 64:(cp + 1) * 64, h * 128:(h + 1) * 128], lhsT=attb[cp * 64:(cp + 1) * 64, t, h, :],
                  rhs=gv[cp * 64:(cp + 1) * 64, t, h * 128:(h + 1) * 128], start=True, stop=True, r=[("att", t, par), "gv"], w=[oik])
        for par in range(2):
            hp = par * 64
            for cp in range(2):
                c = 2 * t + cp
                tok = slice(t * 128 + cp * 64, t * 128 + cp * 64 + 64)
                for b in range(2):
                    A("pe", "matmul", out=oeb[par][cp * 64:(cp + 1) * 64, b * 128:(b + 1) * 128], lhsT=qg[b][hp:hp + 64, tok], rhs=S_all[b][hp:hp + 64, c, :],
                      start=True, stop=True, r=[("qg", b, t // 4), ("S", b, c)], w=[oek[par]])

    def s1(t):
        d = T[t]
        oib, oik = d["oi"]; oeb = d["oeb"]; oek = d["oek"]; gr, grk = d["gr"]
        oe, oekk = oesr.next(); d["osum"] = osr.next(); d["gn"] = gnr.next()
        osum, osk = d["osum"]; gn, gnk = d["gn"]
        for par in range(2):
            A("act", "activation", out=oe[:, :].rearrange("p (b q c) -> p b q c", q=2, c=128)[:, :, par, :],
              in_=oeb[par][:, 0:256].rearrange("p (b c) -> p b c", c=128), func=AF.Copy, r=[oek[par]], w=[(oekk, par)])
        A("dve", "tensor_tensor", out=osum[:, :], in0=oib[:, :], in1=oe[:, :], op=ALU.add, r=[oik, (oekk, 0), (oekk, 1)], w=[osk])
        A("pool", "tensor_tensor", out=gn[:, :, :], in0=gr[:, :].rearrange("p (h c) -> p h c", c=128),
          in1=gnb[:, :].unsqueeze(1).to_broadcast([128, 4, 128]), op=ALU.mult, r=[grk, "gnb"], w=[gnk])

    def s2(t):
        d = T[t]
        osum, osk = d["osum"]
        d["ss"] = ssr.next(); ss, ssk = d["ss"]
        for h in range(4):
            jt, jk = junkr.next()
            A("act", "activation", out=jt[:, :], in_=osum[:, h * 128:(h + 1) * 128], func=AF.Square, accum_out=ss[:, h:h + 1],
              r=[osk], w=[jk, (ssk, 0, h)])
        A("act", "activation", out=ss[:, 4:8], in_=ss[:, 0:4], func=AF.Ln, scale=1.0 / 128, bias=eps_t[:, 0:1],
          r=[(ssk, 0, h) for h in range(4)] + ["eps"], w=[(ssk, 1)])
        A("act", "activation", out=ss[:, 8:12], in_=ss[:, 4:8], func=AF.Exp, scale=-0.5, r=[(ssk, 1)], w=[(ssk, 2)])

    def s3(t):
        d = T[t]
        osum, osk = d["osum"]; gn, gnk = d["gn"]; ss, ssk = d["ss"]
        d["og"] = ogr.next(); og, ogk = d["og"]
        for h in range(4):
            A("dve", "scalar_tensor_tensor", out=og[:, h * 128:(h + 1) * 128], in0=osum[:, h * 128:(h + 1) * 128], scalar=ss[:, 8 + h:9 + h],
              in1=gn[:, h, :], op0=ALU.mult, op1=ALU.mult, r=[osk, (ssk, 2), gnk], w=[ogk])

    def s4(t):
        og, ogk = T[t]["og"]
        tp, tpk = tps.next()
        for fc in range(4):
            A("pe", "transpose", out=tp[:, fc * 128:(fc + 1) * 128], in_=og[:, fc * 128:(fc + 1) * 128], identity=ident[:, :], r=[ogk, "ident"], w=[tpk])
        ot, otk = otr.next()
        A("act", "activation", out=ot[:, :, :], in_=tp[:, 0:512].rearrange("p (f c) -> p f c", c=128), func=AF.Copy, r=[tpk], w=[otk])
        A("act", "dma_start", out=SC["oglaT"][:, t * 128:(t + 1) * 128].rearrange("(f p) c -> p f c", p=128), in_=ot[:, :, :], r=[otk], dma=otk)
        T[t].clear()

    stages = (s0, s1, s2, s3, s4)
    for step in range(NT + len(stages) - 1):
        for si, fn in enumerate(stages):
            t = step - si
            if 0 <= t < NT: fn(t)

def alloc_gla(cx):
    qg = [cx.lsb([128, S], BF16, "qg") for _ in range(2)]
    kg = [cx.lsb([128, S], BF16, "kg") for _ in range(2)]
    dec = [cx.lsb([128, 64], F32, "dec") for _ in range(2)]
    gv = cx.lsb([128, NT, 512], BF16, "gv")
    kgT = cx.lsb([128, NT, 256], BF16, "kgTM")
    cx.gla = (qg, kg, dec, gv, kgT)


def alloc_out_weights(cx):
    cx.wout = (cx.lsb([128, 4, 1024], BF16, "wun"), cx.lsb([128, 4, 1024], BF16, "wug"), cx.lsb([128, 8, 1024], BF16, "wo"))


def phase_out(cx, l, xsrc, dst, W, C, SC):
    A = cx.S.add
    eps_t = cx.sb([128, 1], F32, "eps")
    A("pool", "memset", ap=eps_t[:, :], constant=EPS, w=["eps"])
    gbc = cx.sb([128, D], F32, "gbc")
    A("sp", "dma_start", out=gbc[:, :], in_=W["g_post"][l:l + 1, :].partition_broadcast(128)[:, 0, :], w=["gbc"], dma="gbc")
    wun, wug, wo = cx.wout
    inr = cx.ring(2, [128, 4, 512], BF16, "on"); igr = cx.ring(2, [128, 4, 512], BF16, "og")
    mr = cx.ring(8, [128, 512], BF16, "mg")
    ups = cx.psring(4, [128, 512], F32, "ups")
    ops_ = cx.psring(4, [128, 512], F32, "ops")
    t1r = cx.ring(3, [128, 512], F32, "t1"); t2r = cx.ring(3, [128, 512], F32, "t2")
    yT = cx.ring(2, [128, 8, 512], BF16, "yT")
    ssr = cx.ring(4, [128, 8], F32, "ss"); junk = cx.sb([128, 512], BF16, "junk")
    xr = cx.ring(3, [128, D], F32, "x"); rr = cx.ring(3, [128, D], F32, "r")
    ys = {}

    def emit_up(j):
        js = slice(j * 512, (j + 1) * 512)
        on, onk = inr.next(); og, ogk = igr.next(); y, yk = yT.next()
        A("sp", "dma_start", out=on[:, :, :], in_=SC["onsaT"][:, js].rearrange("(k p) c -> p k c", p=128), w=[onk], dma=onk)
        A("sp", "dma_start", out=og[:, :, :], in_=SC["oglaT"][:, js].rearrange("(k p) c -> p k c", p=128), w=[ogk], dma=ogk)
        for cb in range(8):
            un, unk = ups.next(); ug, ugk = ups.next()
            for kc in range(4):
                A("pe", "matmul", out=un[:, :], lhsT=wun[:, kc, cb * 128:(cb + 1) * 128], rhs=on[:, kc, :], start=(kc == 0), stop=(kc == 3),
                  r=["w_up_nsa", onk], w=[unk])
            for kc in range(4):
                A("pe", "matmul", out=ug[:, :], lhsT=wug[:, kc, cb * 128:(cb + 1) * 128], rhs=og[:, kc, :], start=(kc == 0), stop=(kc == 3),
                  r=["w_up_gla", ogk], w=[ugk])
            mn, mnk = mr.next(); mg, mgk = mr.next()
            A("sp", "dma_start", out=mn[:, :], in_=SC["mgT"][cb * 128:(cb + 1) * 128, js], w=[mnk], dma=mnk)
            A("sp", "dma_start", out=mg[:, :], in_=SC["mgT"][1024 + cb * 128:1024 + (cb + 1) * 128, js], w=[mgk], dma=mgk)
            t1, t1k = t1r.next(); t2, t2k = t2r.next()
            A("dve", "tensor_tensor", out=t1[:, :], in0=un[:, :], in1=mn[:, :], op=ALU.mult, r=[unk, mnk], w=[t1k])
            A("dve", "tensor_tensor", out=t2[:, :], in0=ug[:, :], in1=mg[:, :], op=ALU.mult, r=[ugk, mgk], w=[t2k])
            A("pool", "tensor_tensor", out=y[:, cb, :], in0=t1[:, :], in1=t2[:, :], op=ALU.add, r=[t1k, t2k], w=[(yk, cb)])
        ys[j] = (y, yk)

    def emit_out(j):
        y, yk = ys[j]
        for qs in range(4):
            t = 4 * j + qs
            ss, ssk = ssr.next(); xt, xk = xr.next(); rt, rk = rr.next()
            A("sp", "dma_start", out=xt[:, :], in_=xsrc[t * 128:(t + 1) * 128, :], w=[xk], dma=xk)
            ob = []
            for half in range(2):
                o, ok = ops_.next()
                for cb in range(8):
                    A("pe", "matmul", out=o[:, :], lhsT=y[:, cb, qs * 128:(qs + 1) * 128], rhs=wo[:, cb, half * 512:(half + 1) * 512],
                      start=(cb == 0), stop=(cb == 7), r=[(yk, cb), "w_out"], w=[ok])
                A("act", "activation", out=junk[:, :], in_=o[:, :], func=AF.Square, accum_out=ss[:, half:half + 1], r=[ok], w=["junk", (ssk, half)])
                ob.append((o, ok))
            A("dve", "tensor_tensor", out=ss[:, 2:3], in0=ss[:, 0:1], in1=ss[:, 1:2], op=ALU.add, r=[(ssk, 0), (ssk, 1)], w=[(ssk, 2)])
            A("act", "activation", out=ss[:, 3:4], in_=ss[:, 2:3], func=AF.Ln, scale=1.0 / D, bias=eps_t[:, 0:1], r=[(ssk, 2), "eps"], w=[(ssk, 3)])
            A("act", "activation", out=ss[:, 4:5], in_=ss[:, 3:4], func=AF.Exp, scale=-0.5, r=[(ssk, 3)], w=[(ssk, 4)])
            for half, (o, ok) in enumerate(ob):
                hs = slice(half * 512, (half + 1) * 512)
                A("dve", "scalar_tensor_tensor", out=rt[:, hs], in0=o[:, :], scalar=ss[:, 4:5], in1=gbc[:, hs], op0=ALU.mult, op1=ALU.mult,
                  r=[ok, (ssk, 4), "gbc"], w=[(rk, half)])
            A("pool", "tensor_tensor", out=rt[:, :], in0=rt[:, :], in1=xt[:, :], op=ALU.add, r=[(rk, 0), (rk, 1), xk], w=[(rk, 0), (rk, 1)])
            A("pool", "dma_start", out=dst[t * 128:(t + 1) * 128, :], in_=rt[:, :], r=[(rk, 0), (rk, 1)], dma=rk)

    emit_up(0)
    for j in range(NQ):
        if j + 1 < NQ: emit_up(j + 1)
        emit_out(j)


def prefetch_out_weights(cx, l, W):
    A = cx.S.add
    wun, wug, wo = cx.wout
    stg = cx.ring(2, [128, 4, 1024], F32, "wstg")
    for (wt, nm, k0) in ((wun, "w_up_nsa", 0), (wug, "w_up_gla", 0), (wo, "w_out", 0), (wo, "w_out", 4)):
        s, sk = stg.next()
        A("act", "dma_start", out=s[:, :, :], in_=W[nm][l][k0 * 128:(k0 + 4) * 128, :].rearrange("(k p) c -> p k c", p=128), w=[sk], dma=sk)
        A("pool", "tensor_copy", out=wt[:, k0:k0 + 4, :], in_=s[:, :, :], r=[sk], w=[(nm, k0)])


WSPEC = [("g_pre", [DEPTH, D]), ("w_in", [DEPTH, D, DIN]),
         ("cmp_pos_k", [DEPTH, 64, 32]), ("cmp_w1_k", [DEPTH, 64, 32, 128]), ("cmp_w2_k", [DEPTH, 128, 64]),
         ("cmp_pos_v", [DEPTH, 64, 32]), ("cmp_w1_v", [DEPTH, 64, 32, 128]), ("cmp_w2_v", [DEPTH, 128, 64]),
         ("gla_w_a", [DEPTH, 16, 256]), ("gla_b_a", [DEPTH, 256, 1]), ("gla_g_norm", [DEPTH, 128]),
         ("w_up_nsa", [DEPTH, 512, D]), ("w_up_gla", [DEPTH, 512, D]), ("w_out", [DEPTH, D, D]), ("g_post", [DEPTH, D])]
SCSPEC = [("qrT", [8, 64, S], BF16), ("qnT", [8, 64, S], BF16), ("ksT", [2, 64, S], BF16), ("kwT", [2, 64, S], BF16),
          ("kcT", [2, 64, S], BF16), ("vcT", [2, 64, S], BF16), ("gqT", [256, S], BF16), ("gkT", [256, S], BF16),
          ("gaT", [16, S], BF16), ("mgT", [2048, S], BF16), ("vsw", [S, 260], BF16), ("gate", [S, 24], F32),
          ("nzs", [S, 512], BF16), ("gv", [S, 512], BF16), ("grs", [S, 512], BF16),
          ("kcmpT", [2, 64, 256], BF16), ("vcmp", [2, 256, 64], BF16), ("negT", [2, 64, S], BF16),
          ("ocmp", [S, 512], F32), ("onsaT", [512, S], BF16), ("oglaT", [512, S], BF16), ("resid", [S, D], F32)]


def make_consts():
    import ml_dtypes
    bf = ml_dtypes.bfloat16
    c = {}
    pos = np.arange(S, dtype=np.float32)
    inv_freq = (10000.0 ** (-np.arange(0, 64, 2, dtype=np.float32) / 64)).astype(np.float32)
    ang = pos[:, None] * inv_freq[None, :]
    cos, sin = np.cos(ang).astype(np.float32), np.sin(ang).astype(np.float32)
    c["cosT"] = np.ascontiguousarray(np.concatenate([cos, cos, cos, cos], 1).T)
    c["sinS"] = np.ascontiguousarray(np.concatenate([-sin, sin, -sin, sin], 1).T)
    c["ident"] = np.eye(128, dtype=np.float32).astype(bf)
    jc = np.arange(128)[:, None]; tq = np.arange(512)[None, :]
    c["cmask"] = np.stack([np.where(tq - 16 * jc < 31 - 512 * m, -BIG, 0.0) for m in range(5)]).astype(np.float32).astype(bf)
    kk = np.arange(128)[:, None]
    c["amask"] = np.stack([np.where(128 * r + kk > tq, -BIG, 0.0) for r in range(4)] +
                          [np.where(128 * r + kk <= tq, -BIG, 0.0) for r in range(4)]).astype(np.float32).astype(bf)
    c["emat"] = np.where(np.arange(S)[None, :] // 64 == np.arange(64)[:, None], BIG, 0.0).astype(np.float32).astype(bf)
    cs = 16 * np.arange(256)[:, None]; bs = 64 * np.arange(64)[None, :]
    ovl = ((cs < bs + 64) & (cs + 32 > bs)).astype(np.float32); ovl[255] = 0
    c["overlap"] = ovl.astype(bf)
    t = np.arange(S)[:, None]; n = np.arange(64)[None, :]; cur = t // 64
    forced = (n == 0) | (n == cur) | (n == cur - 1)
    c["rst"] = np.tile((np.arange(512) % 64 != 0).astype(np.float32)[None, :], (128, 1))
    c["tri"] = (np.arange(128)[:, None] % 64 <= np.arange(64)[None, :]).astype(np.float32).astype(bf)
    c["fbias"] = np.where(forced, 1.0e4, np.where(n <= cur, 0.0, -1.0e4)).astype(np.float32)
    return c


CSPEC = [("cosT", [128, S], F32), ("sinS", [128, S], F32), ("ident", [128, 128], BF16), ("cmask", [5, 128, 512], BF16),
         ("amask", [8, 128, 512], BF16), ("emat", [64, S], BF16), ("overlap", [256, 64], BF16), ("fbias", [S, 64], F32),
         ("rst", [128, 512], F32), ("tri", [128, 64], BF16)]


def build(nc, debug=None):
    debug = debug or {}
    dump = set(debug.get("dump", ()))
    stop = debug.get("stop", None)
    x = nc.dram_tensor("x", [S, D], F32, kind="ExternalInput").ap()
    W = {n: nc.dram_tensor(n, shp, F32, kind="ExternalInput").ap() for n, shp in WSPEC}
    C = {n: nc.dram_tensor(n, shp, dt, kind="ExternalInput").ap() for n, shp, dt in CSPEC}
    y = nc.dram_tensor("y", [S, D], F32, kind="ExternalOutput").ap()
    SC = {n: nc.dram_tensor("sc_" + n, shp, dt, kind=("ExternalOutput" if n in dump else "Internal")).ap()
          for n, shp, dt in SCSPEC}
    with ExitStack() as es:
        cx = Ctx(nc, es)
        cx.dbg = debug

        def phase(name, fn, *a):
            with ExitStack() as pes:
                cx.pes = pes
                fn(cx, *a)
                cx.S.emit(name)
        for l in range(DEPTH):
          with ExitStack() as les:
            cx.les = les
            xsrc = x if l == 0 else SC["resid"]
            phase("proj%d" % l, phase_proj, l, xsrc, W, C, SC)
            if stop == "proj": break
            alloc_out_weights(cx)
            phase("cmpr%d" % l, phase_compress, l, W, C, SC)
            if stop == "cmpr": break
            with ExitStack() as aes:
                cx.aes = aes
                alloc_attn_consts(cx)
                phase("cmp%d" % l, phase_cmp, l, W, C, SC)
                if stop != "cmp":
                    phase("attn%d" % l, phase_attn, l, W, C, SC)
            if stop in ("cmp", "attn"): break
            alloc_gla(cx)
            phase("glaa%d" % l, phase_gla_a, l, W, C, SC)
            phase("glab%d" % l, phase_gla_b, l, W, C, SC)
            if stop == "gla": break
            phase("out%d" % l, phase_out, l, xsrc, (SC["resid"] if l < DEPTH - 1 else y), W, C, SC)
            if stop == "out": break
    return nc


def prep_inputs(inputs):
    f = lambda a: np.ascontiguousarray(np.asarray(a, dtype=np.float32))
    w = {k: f(v) for k, v in inputs.items() if k != "x"}
    w["cmp_pos_k"] = f(w["cmp_pos_k"].transpose(0, 2, 1))
    w["cmp_pos_v"] = f(w["cmp_pos_v"].transpose(0, 2, 1))
    w["cmp_w1_k"] = f(w["cmp_w1_k"].transpose(0, 2, 1, 3))
    w["cmp_w1_v"] = f(w["cmp_w1_v"].transpose(0, 2, 1, 3))
    w["gla_b_a"] = f(w["gla_b_a"][:, :, None])
    w.update(make_consts())
    xs = f(inputs["x"])
    return [dict(w, x=xs[b]) for b in range(xs.shape[0])]


def kernel(**inputs):
    in_maps = prep_inputs(inputs)
    nc = bass.Bass("TRN2", target_bir_lowering=False)
    build(nc)
    res = run_bass_kernel_spmd(nc, in_maps, core_ids=list(range(8)))
    return np.stack([np.asarray(r["y"], dtype=np.float32) for r in res.results], axis=0)
```
